# Optimizing a Trainium2 kernel written in Bass

```python
import math
import jax, jax.numpy as jnp
from jax import lax
import numpy as np

D_MODEL = 1024
BATCH = 16
SEQ = 2048
DEPTH = 2

HEAD_DIM = 64
NORM_EPS = 1e-6
A_HEADS = 8
A_WIDTH = A_HEADS * HEAD_DIM
LORA_W = 64
LORA_A = 64
LNX_EPS = 64e-5
B_GROUPS = 8
B_GROUP_DIM = 64
B_WIDTH = B_GROUPS * B_GROUP_DIM
CHUNK = 128
GMLP_LN_EPS = 1e-5
EVEN_MIX = A_WIDTH + B_WIDTH
SHIFT_W = 3 * A_WIDTH + LORA_W + LORA_A
EVEN_IN = SHIFT_W + 2 * B_WIDTH + EVEN_MIX
C_HEADS = 16
C_KV_HEADS = 2
C_GROUP = C_HEADS // C_KV_HEADS
WINDOW = 128
ODD_MIX = C_HEADS * HEAD_DIM
KV_W = C_KV_HEADS * HEAD_DIM
ODD_IN = ODD_MIX + 2 * KV_W + ODD_MIX
N_EVEN = (DEPTH + 1) // 2
N_ODD = DEPTH // 2

kernel_name = "hybrid_rwkv7_gmlp_swa_adaln"


def rmsnorm(x, g, eps=NORM_EPS):
    xf = x.astype(jnp.float32)
    r = lax.rsqrt(jnp.mean(xf * xf, axis=-1, keepdims=True) + eps)
    return (xf * r).astype(x.dtype) * g


def layernorm(x, g, b, eps):
    xf = x.astype(jnp.float32)
    mu = jnp.mean(xf, axis=-1, keepdims=True)
    var = jnp.mean(jnp.square(xf - mu), axis=-1, keepdims=True)
    return ((xf - mu) * lax.rsqrt(var + eps)).astype(x.dtype) * g + b


def token_shift(p):
    return jnp.pad(p, ((0, 0), (1, 0), (0, 0)))[:, :-1]


def alibi_slopes(n_heads):
    return jnp.asarray(2.0 ** (-8.0 * np.arange(1, n_heads + 1, dtype=np.float32) / n_heads), jnp.float32)


def rwkv7_time_mix(p, mu, w0, w_up, a0, a_up, k_k, k_a, r_k, lnx_g, lnx_b):
    bsz, t_len, _ = p.shape
    p = p + mu * (token_shift(p) - p)
    r, k, v, w_lo, a_lo = jnp.split(
        p, [A_WIDTH, 2 * A_WIDTH, 3 * A_WIDTH, 3 * A_WIDTH + LORA_W], axis=-1)
    w = -jax.nn.softplus(-(w0 + jnp.tanh(w_lo) @ w_up)) - 0.5
    decay = jnp.exp(-jnp.exp(w.astype(jnp.float32))).astype(p.dtype)
    a = jax.nn.sigmoid(a0 + a_lo @ a_up)

    def heads(t):
        return t.reshape(bsz, t_len, A_HEADS, HEAD_DIM)

    kk = heads(k * k_k).astype(jnp.float32)
    kk = (kk / jnp.maximum(jnp.sqrt(jnp.sum(kk * kk, axis=-1, keepdims=True)), 1e-12)).astype(p.dtype)
    k = k * (1 + (a - 1) * k_a)
    r, k, v, decay, a = heads(r), heads(k), heads(v), heads(decay), heads(a)

    def step(S, inp):
        r_t, w_t, k_t, v_t, kk_t, a_t = inp
        sa = jnp.einsum('bhvk,bhk->bhv', S, -kk_t)
        S = (S * w_t[:, :, None, :]
             + sa[..., None] * (kk_t * a_t)[:, :, None, :]
             + v_t[..., None] * k_t[:, :, None, :])
        y_t = jnp.einsum('bhvk,bhk->bhv', S, r_t)
        return S, y_t

    xs = tuple(jnp.moveaxis(t, 1, 0) for t in (r, decay, k, v, kk, a))
    S0 = jnp.zeros((bsz, A_HEADS, HEAD_DIM, HEAD_DIM), jnp.float32)
    _, y = lax.scan(step, S0, xs)
    y = jnp.moveaxis(y, 0, 1).astype(p.dtype)
    y = layernorm(y, lnx_g.reshape(A_HEADS, HEAD_DIM), lnx_b.reshape(A_HEADS, HEAD_DIM), LNX_EPS)
    y = y + jnp.sum(r * k * r_k, axis=-1, keepdims=True) * v
    return y.reshape(bsz, t_len, A_WIDTH)


def chunked_spatial_gate(u, v, ln_g, ln_b, w_s, b_s):
    bsz, t_len, _ = u.shape
    n_chunks = t_len // CHUNK
    v = layernorm(v.reshape(bsz, t_len, B_GROUPS, B_GROUP_DIM),
                  ln_g.reshape(B_GROUPS, B_GROUP_DIM), ln_b.reshape(B_GROUPS, B_GROUP_DIM), GMLP_LN_EPS)
    v = v.reshape(bsz, n_chunks, CHUNK, B_GROUPS, B_GROUP_DIM)
    causal = jnp.tril(jnp.ones((CHUNK, CHUNK), dtype=bool))
    w = jnp.where(causal[None], w_s, jnp.zeros_like(w_s))
    mixed = jnp.einsum('gts,bcsgd->bctgd', w, v) + b_s.T[None, None, :, :, None]
    return u * mixed.reshape(bsz, t_len, B_WIDTH)


def even_mixer(h, w_in, mu, w0, w_up, a0, a_up, k_k, k_a, r_k, lnx_g, lnx_b,
               sg_ln_g, sg_ln_b, sg_w, sg_b, w_out):
    p = h @ w_in
    p_a, u, v, z = jnp.split(p, [SHIFT_W, SHIFT_W + B_WIDTH, SHIFT_W + 2 * B_WIDTH], axis=-1)
    y_a = rwkv7_time_mix(p_a, mu, w0, w_up, a0, a_up, k_k, k_a, r_k, lnx_g, lnx_b)
    y_b = chunked_spatial_gate(u, v, sg_ln_g, sg_ln_b, sg_w, sg_b)
    y = jnp.concatenate([y_a, y_b], axis=-1) * jax.nn.silu(z)
    return y @ w_out


def swa_mixer(h, w_in, sinks, w_out):
    bsz, t_len, _ = h.shape
    nb = t_len // WINDOW
    p = h @ w_in
    q, k, v, z = jnp.split(p, [ODD_MIX, ODD_MIX + KV_W, ODD_MIX + 2 * KV_W], axis=-1)
    q = q.reshape(bsz, nb, WINDOW, C_KV_HEADS, C_GROUP, HEAD_DIM)

    def band(t):
        t = jnp.pad(t.reshape(bsz, t_len, C_KV_HEADS, HEAD_DIM), ((0, 0), (WINDOW, 0), (0, 0), (0, 0)))
        t = t.reshape(bsz, nb + 1, WINDOW, C_KV_HEADS, HEAD_DIM)
        return jnp.concatenate([t[:, :-1], t[:, 1:]], axis=2)

    k_band, v_band = band(k), band(v)
    scores = jnp.einsum('bnqhgd,bnkhd->bnhgqk', q, k_band).astype(jnp.float32) * (HEAD_DIM ** -0.5)
    qi = jnp.arange(WINDOW)[:, None]
    kj = jnp.arange(2 * WINDOW)[None, :]
    dist = qi + WINDOW - kj
    key_pos = (jnp.arange(nb)[:, None, None] - 1) * WINDOW + kj[None]
    valid = (dist >= 0)[None] & (dist < WINDOW)[None] & (key_pos >= 0)
    slopes = alibi_slopes(C_HEADS).reshape(C_KV_HEADS, C_GROUP)
    scores = scores - slopes[None, None, :, :, None, None] * dist.astype(jnp.float32)
    scores = jnp.where(valid[None, :, None, None], scores, -jnp.inf)
    sink = sinks.reshape(C_KV_HEADS, C_GROUP).astype(jnp.float32)[None, None, :, :, None, None]
    m = jnp.maximum(jnp.max(scores, axis=-1, keepdims=True), sink)
    e = jnp.exp(scores - m)
    probs = e / (jnp.sum(e, axis=-1, keepdims=True) + jnp.exp(sink - m))
    out = jnp.einsum('bnhgqk,bnkhd->bnqhgd', probs.astype(v_band.dtype), v_band)
    y = out.reshape(bsz, t_len, ODD_MIX) * jax.nn.silu(z)
    return y @ w_out


def setup_inputs(seed: int = 0) -> dict:
    key = jax.random.key(seed)
    ks = jax.random.split(key, 32)
    f32 = jnp.float32
    nrm = lambda k, shape, s: (jax.random.normal(k, shape, f32) * s).astype(f32)
    D = D_MODEL
    return {
        "x": nrm(ks[0], (BATCH, SEQ, D), 1.0),
        "c": nrm(ks[1], (BATCH, D), 1.0),
        "ada_w": nrm(ks[2], (DEPTH, D, 3 * D), 0.2 * D ** -0.5),
        "ada_b": nrm(ks[3], (DEPTH, 3 * D), 0.1),
        "norm_g": 1.0 + nrm(ks[4], (DEPTH, D), 0.02),
        "e_w_in": nrm(ks[5], (N_EVEN, D, EVEN_IN), D ** -0.5),
        "e_mu": jax.random.uniform(ks[6], (N_EVEN, SHIFT_W), f32),
        "e_w0": -0.5 + nrm(ks[7], (N_EVEN, A_WIDTH), 0.5),
        "e_w_up": nrm(ks[8], (N_EVEN, LORA_W, A_WIDTH), 0.5 * LORA_W ** -0.5),
        "e_a0": nrm(ks[9], (N_EVEN, A_WIDTH), 0.5),
        "e_a_up": nrm(ks[10], (N_EVEN, LORA_A, A_WIDTH), 0.5 * LORA_A ** -0.5),
        "e_k_k": 0.85 + nrm(ks[11], (N_EVEN, A_WIDTH), 0.05),
        "e_k_a": 1.0 + nrm(ks[12], (N_EVEN, A_WIDTH), 0.05),
        "e_r_k": nrm(ks[13], (N_EVEN, A_HEADS, HEAD_DIM), 0.1),
        "e_lnx_g": 1.0 + nrm(ks[14], (N_EVEN, A_WIDTH), 0.02),
        "e_lnx_b": nrm(ks[15], (N_EVEN, A_WIDTH), 0.02),
        "e_sg_ln_g": 1.0 + nrm(ks[16], (N_EVEN, B_WIDTH), 0.02),
        "e_sg_ln_b": nrm(ks[17], (N_EVEN, B_WIDTH), 0.02),
        "e_sg_w": nrm(ks[18], (N_EVEN, B_GROUPS, CHUNK, CHUNK), CHUNK ** -0.5),
        "e_sg_b": 1.0 + nrm(ks[19], (N_EVEN, B_GROUPS, CHUNK), 0.02),
        "e_w_out": nrm(ks[20], (N_EVEN, EVEN_MIX, D), EVEN_MIX ** -0.5),
        "o_w_in": nrm(ks[21], (N_ODD, D, ODD_IN), D ** -0.5),
        "o_sinks": nrm(ks[22], (N_ODD, C_HEADS), 0.5),
        "o_w_out": nrm(ks[23], (N_ODD, ODD_MIX, D), ODD_MIX ** -0.5),
        "final_g": 1.0 + nrm(ks[24], (D,), 0.02),
    }


def reference(x, c, ada_w, ada_b, norm_g, e_w_in, e_mu, e_w0, e_w_up, e_a0, e_a_up,
              e_k_k, e_k_a, e_r_k, e_lnx_g, e_lnx_b, e_sg_ln_g, e_sg_ln_b, e_sg_w, e_sg_b,
              e_w_out, o_w_in, o_sinks, o_w_out, final_g):
    cond = jax.nn.silu(c)
    for i in range(DEPTH):
        mod = cond @ ada_w[i] + ada_b[i]
        shift, scale, gate = jnp.split(mod[:, None, :], 3, axis=-1)
        h = rmsnorm(x, norm_g[i]) * (1 + scale) + shift
        j = i // 2
        if i % 2 == 0:
            y = even_mixer(h, e_w_in[j], e_mu[j], e_w0[j], e_w_up[j], e_a0[j], e_a_up[j],
                           e_k_k[j], e_k_a[j], e_r_k[j], e_lnx_g[j], e_lnx_b[j],
                           e_sg_ln_g[j], e_sg_ln_b[j], e_sg_w[j], e_sg_b[j], e_w_out[j])
        else:
            y = swa_mixer(h, o_w_in[j], o_sinks[j], o_w_out[j])
        x = x + gate * y
    return rmsnorm(x, final_g)
```

```python
import contextlib
import numpy as np
import concourse.bass as bass
import concourse.mybir as mybir
from concourse.bass_utils import run_bass_kernel_spmd

F32 = mybir.dt.float32
BF16 = mybir.dt.bfloat16
AF = mybir.ActivationFunctionType
ALU = mybir.AluOpType
AX = mybir.AxisListType

ENGS = ("pe", "act", "dve", "pool", "sp")
NCORES = 8
SEQ = 2048
D = 1024
NCH = SEQ // 128
NC_FM = 105
C1 = -0.5 * float(np.exp(-0.5))
DBG_STEPS = 32
DBG_CUT = 99
import os as _os
EVM = 0


class Cut(Exception):
    pass


DUMP = {"on": False, "col": 0, "ap": None, "names": {}}


def dump(S, name, ap, key, n, bf=False):
    if not DUMP["on"]:
        return
    c0 = DUMP["col"]
    DUMP["names"][name] = (c0, n)
    DUMP["col"] += n
    rid = S.dma(DMA(DUMP["ap"][:, c0:c0 + n], ap), reads=[key] if not isinstance(key, list) else key, writes=[("dbg", name)],
                chan="dbg_" + name, eng="pool")
    DUMP["ops"].append(rid)


def cp(n):
    if n == DBG_CUT:
        raise Cut()


class Sched:
    LAT = 0.25

    def __init__(self, nc):
        self.nc = nc
        self.ops = []
        self.rec = []
        self.seen = {e: {} for e in ENGS}
        self.domain_ops = {}
        self.rec2op = {}
        self.nrec = 0
        self.reorder = True
        self.plan = None
        self.filler = None
        self.bank_plans = None
        self.seg = 0

    def add(self, eng, fn, reads=(), writes=(), chan=None, extra=()):
        rid = self.nrec
        self.nrec += 1
        cost = getattr(fn, "cost", None)
        if cost is None:
            cost = {"pe": 0.1, "act": 0.5, "dve": 0.5, "pool": 0.8, "sp": 2.0}[eng]
        if eng == "pool" and chan is None:
            cost = cost * 2.0
        if chan is not None:
            cost = max(cost, 2.0)
        self.rec.append(dict(rid=rid, eng=eng, fn=fn, reads=list(reads), writes=list(writes), chan=chan, cost=cost))
        return rid

    def pe(self, fn, reads=(), writes=()):
        return self.add("pe", fn, reads, writes)

    def act(self, fn, reads=(), writes=()):
        return self.add("act", fn, reads, writes)

    def dve(self, fn, reads=(), writes=()):
        return self.add("dve", fn, reads, writes)

    def pool(self, fn, reads=(), writes=()):
        return self.add("pool", fn, reads, writes)

    def dma(self, fn, reads=(), writes=(), chan="c0", eng="sp"):
        return self.add(eng, fn, reads, writes, chan=chan)

    def dp(self, fn, reads=(), writes=()):
        rid = self.add("pool", fn, reads, writes)
        r = self.rec[-1]
        r["alts"] = {"pool": (fn, r["cost"]), "dve": (fn, r["cost"] / 2.0)}
        return rid

    def copy(self, out, in_, reads=(), writes=()):
        fa, fd = ACTF(out, in_, AF.Copy), CP(out, in_)
        rid = self.add("act", fa, reads, writes)
        self.rec[-1]["alts"] = {"act": (fa, fa.cost), "dve": (fd, fd.cost)}
        return rid

    def _flush(self):
        import heapq
        recs = self.rec
        self.rec = []
        n = len(recs)
        if n == 0:
            return
        lastw, readers = {}, {}
        preds = [set() for _ in range(n)]
        for i, r in enumerate(recs):
            reads, writes = r["reads"], r["writes"]
            excl = [k for k in reads if isinstance(k, tuple) and k[0] in ("pb", "pbj")]
            if excl:
                reads = [k for k in reads if k not in excl]
                writes = writes + excl
            d = preds[i]
            for k in reads:
                if k in lastw:
                    d.add(lastw[k])
            for k in writes:
                if RELAX and (k if isinstance(k, str) else k[0]) in RELAX:
                    continue
                if k in lastw:
                    d.add(lastw[k])
                d.update(readers.get(k, ()))
            d.discard(i)
            for k in reads:
                readers.setdefault(k, []).append(i)
            for k in writes:
                lastw[k] = i
                readers[k] = []
        order = list(range(n))
        if self.plan is None and self.bank_plans:
            assign, acq = self.bank_plans[self.seg]
            self.seg += 1
            self._chain_banks(recs, preds, assign, acq)
        if self.plan is not None:
            self.plan.append(self._plan_banks(recs, preds))
            return
        if self.reorder:
            order, mk = self._listsched(recs, preds)
            self.last_makespan = mk
            print("[sched] segment ops=%d simulated makespan=%.1f us" % (n, mk))
        fills = {}
        if self.reorder and self.filler is not None and FILL_FRAC > 0:
            start, fin = self._last_times
            prev = None
            nf = 0
            for i in order:
                if recs[i]["eng"] != "pe" or recs[i]["chan"] is not None:
                    continue
                if prev is not None:
                    gap = start[i] - fin[prev]
                    if FILL_MIN < gap < FILL_MAX:
                        k = int(FILL_FRAC * gap / FILL_COST)
                        if k > 0:
                            fills[prev] = k
                            nf += k
                prev = i
            print("[sched] inserted %d PE warm-keeping filler matmuls" % nf)
        pos_of = {}
        for i in order:
            r = recs[i]
            idx = self._commit(r, [pos_of[p] for p in preds[i]])
            pos_of[i] = idx
            self.rec2op[r["rid"]] = idx
            for _ in range(fills.get(i, 0)):
                self._commit(dict(eng="pe", fn=self.filler, chan=None), [])

    @staticmethod
    def _chain_banks(recs, preds, assign, acq):
        first, last = {}, {}
        for i, r in enumerate(recs):
            for k in r["reads"] + r["writes"]:
                if isinstance(k, tuple) and k[0] == "pbj":
                    first.setdefault(k[1], i)
                    last[k[1]] = i
        prev_on = {}
        for j in acq:
            b = assign[j]
            if b in prev_on:
                preds[first[j]].add(last[prev_on[b]])
            prev_on[b] = j

    def _listsched(self, recs, preds):
        import heapq
        n = len(recs)
        succs = [[] for _ in range(n)]
        for i in range(n):
            for p in preds[i]:
                succs[p].append(i)
        prio = [0.0] * n
        for i in range(n - 1, -1, -1):
            m = 0.0
            for q in succs[i]:
                if prio[q] > m:
                    m = prio[q]
            prio[i] = recs[i]["cost"] + self.LAT + m
        indeg = [len(preds[i]) for i in range(n)]
        ready_t = [0.0] * n
        fin = [0.0] * n
        heaps = {e: [] for e in ENGS}

        def push(i):
            alts = recs[i].get("alts")
            for e_ in (alts if alts else (recs[i]["eng"],)):
                heapq.heappush(heaps[e_], (-prio[i], i))
        for i in range(n):
            if indeg[i] == 0:
                push(i)
        free = {e: 0.0 for e in ENGS}
        start = [0.0] * n
        done = 0
        while done < n:
            best = None
            for e in ENGS:
                h = heaps[e]
                if not h:
                    continue
                cand = heapq.nsmallest(LS_K, h)
                tb, cb = None, None
                for c in cand:
                    t = max(free[e], ready_t[c[1]])
                    if tb is None or t < tb - LS_EPS:
                        tb, cb = t, c
                if best is None or tb < best[0] - 1e-9:
                    best = (tb, e, cb)
            t, e, c = best
            i = c[1]
            r = recs[i]
            for e_ in (r["alts"] if r.get("alts") else (e,)):
                heaps[e_].remove(c)
                heapq.heapify(heaps[e_])
            if r.get("alts"):
                r["eng"] = e
                r["fn"], r["cost"] = r["alts"][e]
            start[i] = t
            if r["chan"] is not None:
                free[e] = t + 0.1
                fin[i] = t + r["cost"]
            else:
                free[e] = t + r["cost"]
                fin[i] = free[e]
            done += 1
            for q in succs[i]:
                rt = fin[i] + (self.LAT if recs[q]["eng"] != e or r["chan"] is not None else (0.0 if e == "pe" else 0.05))
                if rt > ready_t[q]:
                    ready_t[q] = rt
                indeg[q] -= 1
                if indeg[q] == 0:
                    push(q)
        order = sorted(range(n), key=lambda i: (start[i], i))
        self._last_times = (start, fin)
        return order, max(fin)

    def _plan_banks(self, recs, preds, nbanks=7):
        import heapq
        n = len(recs)
        jobs_of = []
        job_ops = {}
        for i, r in enumerate(recs):
            js = sorted({k[1] for k in r["reads"] + r["writes"] if isinstance(k, tuple) and k[0] == "pbj"})
            jobs_of.append(js)
            for j in js:
                job_ops[j] = job_ops.get(j, 0) + 1
        succs = [[] for _ in range(n)]
        for i in range(n):
            for p in preds[i]:
                succs[p].append(i)
        prio = [0.0] * n
        for i in range(n - 1, -1, -1):
            m = 0.0
            for q in succs[i]:
                if prio[q] > m:
                    m = prio[q]
            prio[i] = recs[i]["cost"] + self.LAT + m
        def run(nbanks, all_jobs, inorder=True):
            starts = []
            indeg = [len(preds[i]) for i in range(n)]
            ready_t = [0.0] * n
            fin = [0.0] * n
            heaps = {e: [] for e in ENGS}
            for i in range(n):
                if indeg[i] == 0:
                    heapq.heappush(heaps[recs[i]["eng"]], (-prio[i], i))
            free = {e: 0.0 for e in ENGS}
            bank_free = [0.0] * min(nbanks, 4096)
            INF = float("inf")
            assign = {}
            job_left = dict(job_ops)
            job_fin = {}
            nxt = 0
            done = 0
            while done < n:
                best = None
                for e in ENGS:
                    h = heaps[e]
                    if not h:
                        continue
                    cand = heapq.nsmallest(8, h)
                    tb, cb = None, None
                    for c in cand:
                        i = c[1]
                        t = max(free[e], ready_t[i])
                        new = [j for j in jobs_of[i] if j not in assign]
                        if new:
                            if inorder and (len(new) > 1 or nxt >= len(all_jobs) or new[0] != all_jobs[nxt]):
                                continue
                            bf = min(bank_free)
                            if bf == INF:
                                continue
                            t = max(t, bf)
                        if tb is None or t < tb - 1e-9:
                            tb, cb = t, c
                    if cb is not None and (best is None or tb < best[0] - 1e-9):
                        best = (tb, e, cb)
                if best is None:
                    for e in ENGS:
                        for c in heaps[e]:
                            i = c[1]
                            new = [j for j in jobs_of[i] if j not in assign]
                            if (not new or (len(new) == 1 and nxt < len(all_jobs) and new[0] == all_jobs[nxt])) and \
                                    (not new or min(bank_free) < INF):
                                t = max(free[e], ready_t[i], min(bank_free) if new else 0.0)
                                if best is None or t < best[0]:
                                    best = (t, e, c)
                    assert best is not None, "bank planning deadlock"
                t, e, c = best
                heaps[e].remove(c)
                heapq.heapify(heaps[e])
                i = c[1]
                r = recs[i]
                for j in jobs_of[i]:
                    if j not in assign:
                        b = min(range(len(bank_free)), key=lambda x: bank_free[x])
                        assign[j] = b
                        bank_free[b] = INF
                        starts.append(j)
                        nxt += 1
                if r["chan"] is not None:
                    free[e] = t + 0.1
                    fin[i] = t + r["cost"]
                else:
                    free[e] = t + r["cost"]
                    fin[i] = free[e]
                for j in jobs_of[i]:
                    job_fin[j] = max(job_fin.get(j, 0.0), fin[i])
                    job_left[j] -= 1
                    if job_left[j] == 0:
                        bank_free[assign[j]] = job_fin[j] + self.LAT
                done += 1
                for q in succs[i]:
                    rt = fin[i] + (self.LAT if recs[q]["eng"] != e or r["chan"] is not None else 0.05)
                    if rt > ready_t[q]:
                        ready_t[q] = rt
                    indeg[q] -= 1
                    if indeg[q] == 0:
                        heapq.heappush(heaps[recs[q]["eng"]], (-prio[q], q))
            return assign, max(fin), starts

        _, m0, starts = run(10 ** 6, sorted(job_ops), False)
        best = None
        for acq in (sorted(job_ops), starts):
            try:
                a, m, _ = run(nbanks, acq)
            except AssertionError:
                continue
            print("[plan] ops=%d jobs=%d unconstrained=%.1f planned=%.1f us" % (n, len(job_ops), m0, m))
            pr = [set(p) for p in preds]
            self._chain_banks(recs, pr, a, acq)
            _, mr = self._listsched(recs, pr)
            print("[plan]   -> real list-schedule with this bank plan: %.1f us" % mr)
            if best is None or mr < best[1]:
                best = (a, mr, list(acq))
        return best[0], best[2]

    def _commit(self, r, deps):
        eng, chan = r["eng"], r["chan"]
        idx = len(self.ops)
        op = dict(eng=eng, fn=r["fn"], chan=chan, deps=[], sig=False, idx=idx)
        dom = ("dma", chan) if chan is not None else eng
        op["dom"] = dom
        lst = self.domain_ops.setdefault(dom, [])
        op["pos"] = len(lst) + 1
        lst.append(idx)
        seen = self.seen[eng]
        best = {}
        for d in deps:
            o = self.ops[d]
            if o["dom"] == "pe" and eng == "pe" and chan is None:
                continue
            if seen.get(o["dom"], 0) >= o["pos"]:
                continue
            if best.get(o["dom"], (0, None))[0] < o["pos"]:
                best[o["dom"]] = (o["pos"], d)
        for dom_, (pos, d) in best.items():
            op["deps"].append(d)
            self.ops[d]["sig"] = True
            for k, v in self.ops[d]["vc"].items():
                if seen.get(k, 0) < v:
                    seen[k] = v
        vc = dict(seen)
        vc[dom] = op["pos"]
        if chan is None and eng == "pe":
            seen[dom] = op["pos"]
        op["vc"] = vc
        self.ops.append(op)
        return idx

    def barrier(self):
        self._flush()
        last = [lst[-1] for lst in self.domain_ops.values()]
        for e in ENGS:
            self._commit(dict(eng=e, fn=(lambda eng: None), chan=None), last)

    def emit(self, final_wait_ops=(), sem_stack=None):
        nc = self.nc
        self._flush()
        final_wait_ops = [self.rec2op[r] for r in final_wait_ops]
        if not hasattr(self, "_em"):
            self._em = dict(lo=0, semval={}, cnt={}, sems={}, own=None)
        em = self._em
        if sem_stack is None:
            if em["own"] is None:
                em["own"] = contextlib.ExitStack()
            sem_stack = em["own"]
        lo, hi = em["lo"], len(self.ops)
        semval, cnt, sems = em["semval"], em["cnt"], em["sems"]
        for op in self.ops[lo:hi]:
            if op["chan"] is not None:
                cnt[op["dom"]] = cnt.get(op["dom"], 0) + 16
                semval[op["idx"]] = cnt[op["dom"]]
                op["sig"] = True
            elif op["sig"]:
                cnt[op["dom"]] = cnt.get(op["dom"], 0) + 1
                semval[op["idx"]] = cnt[op["dom"]]
        for d in cnt:
            if d not in sems:
                nm = "s_" + (d if isinstance(d, str) else "dma_" + str(d[1]))
                sems[d] = sem_stack.enter_context(nc.semaphore(nm))
        with nc.Block() as block:
            per_eng = {e: [o for o in self.ops[lo:hi] if o["eng"] == e] for e in ENGS}
            fin = list(final_wait_ops)

            def run(eng_obj, ops, is_last=False):
                for op in ops:
                    for d in op["deps"]:
                        o = self.ops[d]
                        assert d in semval, ("dependency on an op that was emitted without a signal", d)
                        eng_obj.wait_ge(sems[o["dom"]], semval[d])
                    ins = op["fn"](eng_obj)
                    if op["sig"] and ins is not None:
                        ins.then_inc(sems[op["dom"]], 16 if op["chan"] is not None else 1)
                if is_last:
                    for d in fin:
                        o = self.ops[d]
                        eng_obj.wait_ge(sems[o["dom"]], semval[d])

            @block.sync
            def _(e):
                run(e, per_eng["sp"], is_last=True)

            @block.tensor
            def _(e):
                run(e, per_eng["pe"])

            @block.vector
            def _(e):
                run(e, per_eng["dve"])

            @block.scalar
            def _(e):
                run(e, per_eng["act"])

            @block.gpsimd
            def _(e):
                run(e, per_eng["pool"])
        em["lo"] = hi


def _fsz(ap):
    n = 1
    for d in ap.shape[1:]:
        n *= d
    return n


def _wc(fn, cost):
    fn.cost = cost
    return fn


PE_SCALE = 1.0
LS_K = 6
LS_EPS = 1e-9
RELAX = set()


def MM(out, lhsT, rhs, start=True, stop=True):
    c = (0.05 + 0.00045 * _fsz(out)) * (1.6 if lhsT.shape[0] <= 64 else 1.0)
    return _wc(lambda e: e.matmul(out, lhsT=lhsT, rhs=rhs, start=start, stop=stop), max(0.07, c) * PE_SCALE)


def TR(out, in_, ident):
    return _wc(lambda e: e.transpose(out=out, in_=in_, identity=ident), 0.09 * PE_SCALE)


def ACTF(out, in_, func, bias=None, scale=None, accum_out=None):
    kw = {}
    if bias is not None:
        kw["bias"] = bias
    if scale is not None:
        kw["scale"] = scale
    if accum_out is not None:
        kw["accum_out"] = accum_out
    return _wc(lambda e: e.activation(out=out, in_=in_, func=func, **kw), 0.2 + _fsz(out) / 1100.0)


def TT(out, in0, in1, op):
    return _wc(lambda e: e.tensor_tensor(out=out, in0=in0, in1=in1, op=op), 0.12 + _fsz(out) / 950.0)


def TS(out, in0, s1, s2=None, op0=ALU.mult, op1=None):
    c = 0.12 + _fsz(out) / 950.0
    if op1 is None:
        return _wc(lambda e: e.tensor_scalar(out=out, in0=in0, scalar1=s1, scalar2=None, op0=op0), c)
    return _wc(lambda e: e.tensor_scalar(out=out, in0=in0, scalar1=s1, scalar2=s2, op0=op0, op1=op1), c)


def STT(out, in0, scalar, in1, op0, op1):
    return _wc(lambda e: e.scalar_tensor_tensor(out=out, in0=in0, scalar=scalar, in1=in1, op0=op0, op1=op1), 0.12 + _fsz(out) / 500.0)


def CP(out, in_):
    return _wc(lambda e: e.tensor_copy(out=out, in_=in_), 0.12 + _fsz(out) / 950.0)


def RED(out, in_):
    return _wc(lambda e: e.tensor_reduce(out=out, in_=in_, axis=AX.X, op=ALU.add), 0.12 + _fsz(in_) / 950.0)


def DMA(out, in_):
    return _wc(lambda e: e.dma_start(out=out, in_=in_), 2.0 + _fsz(out) * 128 * 4 / 150e3)


def MSET(ap, v):
    return lambda e: e.memset(ap, v)


def bcl(ap, n):
    return bass.AP(ap.tensor, ap.offset, [list(a) for a in ap.ap] + [[0, n]])


def bcm(ap, n):
    a = [list(x) for x in ap.ap]
    return bass.AP(ap.tensor, ap.offset, [a[0], [0, n]] + a[1:])


class Ctx:
    pass


BANK_ASSIGN = {}
NBANK = 7
FILL_FRAC = 0.6
FILL_MIN = 0.5
FILL_MAX = 25.0
FILL_COST = 0.3


def common_setup(nc, S, st, C, dr, layer, ncb, pfx="", nbuf=1, nx=2):
    sb = lambda name, shape, dt=F32: st.enter_context(nc.sbuf_tensor("sb_" + pfx + name, shape, dt))
    C.banks = [st.enter_context(nc.psum_tensor(f"{pfx}pb{i}", [128, 512], F32)) for i in range(NBANK)]
    C.warm = st.enter_context(nc.psum_tensor(pfx + "warm", [128, 512], F32))
    C.bi = 0
    assign = BANK_ASSIGN.get(pfx)

    def bank():
        j = C.bi
        C.bi += 1
        if assign is None:
            return C.banks[j % NBANK], ("pbj", j)
        return C.banks[assign[0][j]], ("pbj", j)
    C.bank = bank
    C.cfm = sb("cfm", [128, NC_FM])
    C.cbc = sb("cbc", [128, ncb])
    C.cT = sb("cT", [128, 8, 2])
    C.condT = sb("condT", [128, 8, 2])
    C.ident = sb("ident", [128, 128])
    C.identb = sb("identb", [128, 128], BF16)
    C.ones = sb("ones", [128, 128])
    C.mhalf = sb("mhalf", [128, 16])
    C.stages = [sb(f"stage{i}", [128, 2, 1024]) for i in range(nbuf)]
    C.stage = C.stages[0]
    C.modrow = sb("modrow", [2, 1024])
    C.modT = sb("modT", [128, 24, 2])
    C.A = sb("Aff", [128, 8, 2])
    C.diag = sb("diag", [128, 2, 128])
    C.gate_bc = sb("gate_bc", [128, 1024])
    C.nx = nx
    C.x = sb("xbuf", [128, nx, 1024])
    C.xns = [sb(f"xn{i}", [128, 1024]) for i in range(nbuf)]
    C.hTs = [sb(f"hT{i}", [128, 8, 128], BF16) for i in range(nbuf)]
    C.ys = [sb(f"ybf{i}", [128, 1024], BF16) for i in range(nbuf)]
    C.yTs = [sb(f"yT{i}", [128, 8, 128], BF16) for i in range(nbuf)]
    C.sms = [sb(f"small{i}", [128, 64]) for i in range(nbuf)]
    C.tmps = [sb(f"tmpx{i}", [128, 512]) for i in range(nbuf)]
    C.xn, C.hT, C.y, C.yT, C.sm, C.tmp = C.xns[0], C.hTs[0], C.ys[0], C.yTs[0], C.sms[0], C.tmps[0]

    S.dma(DMA(C.cfm[:], dr["cfm"]), writes=["cfm"], chan="c_cfm")
    S.dma(DMA(C.cbc[:], dr["cbc"]), writes=["cbc"], chan="c_cbc")
    S.dma(DMA(C.cT[:], dr["cT"]), writes=["cT"], chan="c_cT")
    S.pool(MSET(C.ident[:], 0.0), writes=["ident"])
    S.pool(lambda e: e.affine_select(out=C.ident[:], in_=C.ident[:], pattern=[[-1, 128]], compare_op=ALU.not_equal,
                                     fill=1.0, base=0, channel_multiplier=1), reads=["ident"], writes=["ident"])
    S.dp(CP(C.identb[:], C.ident[:]), reads=["ident"], writes=["identb"])
    S.pool(MSET(C.ones[:], 1.0), writes=["ones"])
    S.pool(MSET(C.mhalf[:], -0.5), writes=["mhalf"])
    S.act(ACTF(C.condT[:], C.cT[:], AF.Tanh, scale=0.5), reads=["cT"], writes=["condT"])
    S.dve(TS(C.condT[:], C.condT[:], 0.5, 0.5, ALU.mult, ALU.add), reads=["condT"], writes=["condT"])
    S.dve(TT(C.condT[:], C.condT[:], C.cT[:], ALU.mult), reads=["condT", "cT"], writes=["condT"])
    adaw = dr["ada_w"]
    cb0 = 8 if layer == 0 else 81
    slots = [(stg, a, h) for stg in C.stages for a in range(2) for h in range(2)]
    n = 0
    for cg in range(6):
        b0, k0 = bank()
        for k in range(8):
            si = n % len(slots)
            stg, a, h = slots[si]
            n += 1
            sv = stg[:, a, h * 512:(h + 1) * 512]
            S.dma(DMA(sv, adaw[k * 128:(k + 1) * 128, cg * 512:(cg + 1) * 512]), writes=[("stage", si)], chan=f"stg{si}")
            S.pe(MM(b0[0:2, :], C.condT[:, k, :], sv, k == 0, k == 7), reads=["condT", ("stage", si)], writes=[k0])
        mr = C.modrow[0:2, (cg % 2) * 512:(cg % 2 + 1) * 512]
        S.act(ACTF(mr, b0[0:2, :], AF.Copy), reads=[k0], writes=[("modrow", cg % 2)])
        bt, kt = bank()
        for q in range(4):
            S.pe(TR(bt[:, 2 * q:2 * q + 2], C.modrow[0:2, (cg % 2) * 512 + q * 128:(cg % 2) * 512 + (q + 1) * 128], C.ident[0:2, 0:2]),
                 reads=[("modrow", cg % 2), "ident"], writes=[kt])
        S.dve(TT(C.modT[:, cg * 4:(cg + 1) * 4, :], bt[:, 0:8].rearrange("p (k s) -> p k s", s=2),
                 bcl(C.cfm[:, cb0 + cg * 4:cb0 + cg * 4 + 4], 2), ALU.add), reads=[kt, "cfm"], writes=["modT"])
    ng0 = 0 if layer == 0 else 73
    S.dve(TS(C.A[:], C.modT[:, 8:16, :], 1.0, None, ALU.add), reads=["modT"], writes=["Aff"])
    S.dve(TT(C.A[:], C.A[:], bcl(C.cfm[:, ng0:ng0 + 8], 2), ALU.mult), reads=["Aff", "cfm"], writes=["Aff"])


def seq_setup(S, C, s):
    bks = [C.bank(), C.bank()]
    for k in range(8):
        sl = k % 2
        S.dve(TS(C.diag[:, sl, :], C.ident[:], C.modT[:, 16 + k, s:s + 1], None, ALU.mult),
              reads=["ident", "modT"], writes=[("diag", sl)])
        b, kb = bks[k // 4]
        S.pe(MM(b[:, (k % 4) * 128:(k % 4 + 1) * 128], C.ones[:], C.diag[:, sl, :]),
             reads=["ones", ("diag", sl)], writes=[kb])
    S.dve(CP(C.gate_bc[:, 0:512], bks[0][0][:]), reads=[bks[0][1]], writes=["gate_bc"])
    S.act(ACTF(C.gate_bc[:, 512:1024], bks[1][0][:], AF.Copy), reads=[bks[1][1]], writes=["gate_bc"])


def norm_and_transpose(S, C, xs, kx, s, par=0):
    xn, hT, sm = C.xns[par], C.hTs[par], C.sms[par]
    ss = sm[:, 0:1]
    S.act(ACTF(xn[:], xs, AF.Square), reads=[kx], writes=[("xn", par)])
    S.dve(RED(ss, xn[:]), reads=[("xn", par)], writes=[("sm0", par)])
    S.dve(TS(sm[:, 1:2], ss, 1.0 / D, 1e-6, ALU.mult, ALU.add), reads=[("sm0", par)], writes=[("sm1", par)])
    S.pool(TT(sm[:, 2:3], sm[:, 1:2], C.mhalf[:, 0:1], ALU.pow), reads=[("sm1", par), "mhalf"], writes=[("sm2", par)])
    S.act(ACTF(xn[:], xs, AF.Identity, scale=sm[:, 2:3]), reads=[kx, ("sm2", par)], writes=[("xn", par)])
    for half in range(2):
        b, kb = C.bank()
        for q in range(4):
            k = half * 4 + q
            S.pe(TR(b[:, q * 128:(q + 1) * 128], xn[:, k * 128:(k + 1) * 128], C.ident[:]),
                 reads=[("xn", par), "ident"], writes=[kb])
        for q in range(4):
            k = half * 4 + q
            if q % 2 == 0:
                S.act(ACTF(hT[:, k, :], b[:, q * 128:(q + 1) * 128], AF.Identity,
                           bias=C.modT[:, k, s:s + 1], scale=C.A[:, k, s:s + 1]),
                      reads=[kb, "modT", "Aff"], writes=[("hT", par, k)])
            else:
                S.dve(TS(hT[:, k, :], b[:, q * 128:(q + 1) * 128], C.A[:, k, s:s + 1], C.modT[:, k, s:s + 1],
                         ALU.mult, ALU.add), reads=[kb, "modT", "Aff"], writes=[("hT", par, k)])


def out_proj_residual(S, C, W, slot, kx, par=0):
    y, yT, tmp = C.ys[par], C.yTs[par], C.tmps[par]
    tb_, tk_ = C.bank()
    tv = tb_[:].bitcast(BF16)
    for k in range(8):
        S.pe(TR(tv[:, k * 128:(k + 1) * 128], y[:, k * 128:(k + 1) * 128], C.identb[:]),
             reads=[("y", par), "identb"], writes=[tk_])
    S.copy(yT[:], tv.rearrange("p (k t) -> p k t", k=8), reads=[tk_], writes=[("yT0", par), ("yT1", par)])
    for cg in range(2):
        b, kb = C.bank()
        for k in range(8):
            S.pe(MM(b[:], yT[:, k, :], W[:, k, cg * 512:(cg + 1) * 512], k == 0, k == 7),
                 reads=[("yT0", par), ("yT1", par), ("W", k)], writes=[kb])
        S.dve(TT(tmp[:], b[:], C.gate_bc[:, cg * 512:(cg + 1) * 512], ALU.mult), reads=[kb, "gate_bc"], writes=[("tmp", par)])
        xv = C.x[:, slot, cg * 512:(cg + 1) * 512]
        S.dp(TT(xv, xv, tmp[:], ALU.add), reads=[("tmp", par), kx], writes=[kx])


def group_ln(S, C, src, ksrc, dst, g_bc, b_bc, eps, pfx, kdst="lnout"):
    sq = C.lnsq
    smt = C.lnsm[pfx]
    s1, s2, m, msq, var, rstd, nmr = (smt[:, 8 * i:8 + 8 * i] for i in range(7))
    src3 = src.rearrange("p (g d) -> p g d", g=8)
    dst3 = dst.rearrange("p (g d) -> p g d", g=8)
    K = lambda n: pfx + n
    S.dve(RED(s1, src3), reads=[ksrc], writes=[K("s1")])
    S.act(ACTF(sq[:], src, AF.Square), reads=[ksrc], writes=[("tmp", 0)])
    S.dve(RED(s2, sq[:].rearrange("p (g d) -> p g d", g=8)), reads=[("tmp", 0)], writes=[K("s2")])
    S.dve(TS(m, s1, 1.0 / 64, None, ALU.mult), reads=[K("s1")], writes=[K("m")])
    S.dve(TT(msq, m, m, ALU.mult), reads=[K("m")], writes=[K("msq")])
    S.dve(STT(var, s2, 1.0 / 64, msq, ALU.mult, ALU.subtract), reads=[K("s2"), K("msq")], writes=[K("var")])
    S.dve(TS(var, var, eps, None, ALU.add), reads=[K("var")], writes=[K("var")])
    S.pool(TT(rstd, var, C.mhalf[:, 0:8], ALU.pow), reads=[K("var"), "mhalf"], writes=[K("rstd")])
    S.dve(STT(nmr, m, -1.0, rstd, ALU.mult, ALU.mult), reads=[K("m"), K("rstd")], writes=[K("nmr")])
    S.dve(TT(dst3, src3, bcl(rstd, 64), ALU.mult), reads=[ksrc, K("rstd")], writes=[kdst])
    S.dp(TT(dst3, dst3, bcl(nmr, 64), ALU.add), reads=[kdst, K("nmr")], writes=[kdst])
    S.dp(TT(dst, dst, g_bc, ALU.mult), reads=[kdst, "cbc"], writes=[kdst])
    S.dp(TT(dst, dst, b_bc, ALU.add), reads=[kdst, "cbc"], writes=[kdst])


def phaseA(nc, S, st, dr, x_out):
    C = Ctx()
    sb = lambda name, shape, dt=F32: st.enter_context(nc.sbuf_tensor("sb_a" + name, shape, dt))
    common_setup(nc, S, st, C, dr, 0, 2048, "a")
    bank = C.bank
    cfm = C.cfm
    W = sb("Win0", [128, 8, 3712], BF16)
    Wo = sb("Wout0", [128, 8, 1024], BF16)
    S.filler = MM(C.warm[:], C.identb[:], W[:, 0, 0:512])
    lora = sb("lora", [128, 512])
    sgw = sb("sgw", [128, 8, 128], BF16)
    mk4 = sb("mk4", [128, 512])
    mkn = sb("mkn", [128, 512])
    ones4 = sb("ones4", [128, 512])
    bones = sb("bones", [128, 128], BF16)
    bo2 = sb("bo2", [128, 2], BF16)
    dc = sb("dconst", [128, 16])
    PA = sb("PA", [128, 13, 129])
    PAprev = sb("PAprev", [128, 13, 1])
    PM = sb("PM", [128, 13, 128])
    TW = sb("TW", [128, 128])
    f4 = lambda name: sb(name, [128, 4, 128])
    th, cs, E1, E2, E3, tha, kx, kp, b2, bt32 = (f4(n) for n in
        ("th", "cs", "E1", "E2", "E3", "tha", "kx", "kp", "b2", "bt32"))
    rn, kt32 = th, cs
    sqb = sb("sqb", [128, 4, 128], BF16)
    AR = sb("AR", [128, 4, 2, 128], BF16)
    btl = sb("btl", [128, 4, 128], BF16)
    ktl = sb("ktl", [128, 4, 128], BF16)
    khb = sb("khb", [128, 4, 128], BF16)
    bhb = sb("bhb", [128, 4, 128], BF16)
    rkb = sb("rkb", [128, 4, 128], BF16)
    kbT = sb("kbT", [128, 1024], BF16)
    khT, bhT = kbT[:, 0:512], kbT[:, 512:1024]
    vT = sb("vT", [128, 512], BF16)
    bonus = sb("bonus", [128, 8])
    MT = sb("MT", [128, 8, 512], BF16)
    Pn = sb("Pn", [128, 2, 8, 128], BF16)
    PT = sb("PTt", [128, 2, 8, 128], BF16)
    TTt = sb("TTt", [128, 8, 128], BF16)
    Rb = sb("Rb", [128, 512], BF16)
    Ub = sb("Ub", [128, 512], BF16)
    S32 = sb("S32", [128, 4, 64])
    Stmp = sb("Stmp", [128, 4, 64])
    SB = sb("SBD", [128, 4, 128], BF16)
    ua = sb("ua", [128, 512])
    vln = sb("vln", [128, 512])
    vlb = sb("vlb", [128, 512], BF16)
    tzt, zst = C.stage, C.stage
    ya = vln
    C.lnsq = C.tmp
    C.lnsm = {"g": sb("lnsm_g", [128, 56]), "a": sb("lnsm_a", [128, 56])}

    S.dma(DMA(lora[:], dr["lora"]), writes=["lora"], chan="c_lora")
    S.dma(DMA(sgw[:], dr["sgwT"]), writes=["sgw"], chan="wq_sg", eng="pool")
    nq = 0
    for k in range(8):
        for c0 in range(0, 3712, 928):
            S.dma(DMA(W[:, k, c0:c0 + 928], dr["e_w_in"][k * 128:(k + 1) * 128, c0:c0 + 928]),
                  writes=[("W0", k, c0), ("wqchain", nq % 4)], chan=f"wq{nq % 4}", eng="pool")
            nq += 1
    for k in range(8):
        S.dma(DMA(Wo[:, k, :], dr["e_w_out"][k * 128:(k + 1) * 128, :]), writes=[("W", k), ("wqchain", nq % 4)], chan=f"wq{nq % 4}", eng="pool")
        nq += 1
    S.pool(MSET(mk4[:], 1.0), writes=["mk4"])
    for q in range(4):
        S.pool(lambda e, q=q: e.affine_select(out=mk4[:, q * 128:(q + 1) * 128], in_=mk4[:, q * 128:(q + 1) * 128],
                                              pattern=[[1, 128]], compare_op=(ALU.is_gt if q % 2 == 0 else ALU.is_ge),
                                              fill=0.0, base=0, channel_multiplier=-1), reads=["mk4"], writes=["mk4"])
    S.pool(MSET(mkn[:], 1.0), writes=["mkn"])
    for q in range(4):
        S.pool(lambda e, q=q: e.affine_select(out=mkn[:, q * 128:(q + 1) * 128], in_=mkn[:, q * 128:(q + 1) * 128],
                                              pattern=[[-1, 128]], compare_op=ALU.is_gt, fill=0.0, base=0,
                                              channel_multiplier=1), reads=["mkn"], writes=["mkn"])
    S.pool(MSET(ones4[:], 1.0), writes=["ones4"])
    S.pool(MSET(ones4[:].rearrange("p (j t) -> p j t", j=4)[:, :, 0:1], 0.0), reads=["ones4"], writes=["ones4"])
    S.pool(MSET(bones[:], 0.0), writes=["bones"])
    S.pool(MSET(bones[0:64, 0:64], 1.0), reads=["bones"], writes=["bones"])
    S.pool(MSET(bones[64:128, 64:128], 1.0), reads=["bones"], writes=["bones"])
    S.pool(MSET(bo2[:], 0.0), writes=["bo2"])
    S.pool(MSET(bo2[0:64, 0:1], 1.0), reads=["bo2"], writes=["bo2"])
    S.pool(MSET(bo2[64:128, 1:2], 1.0), reads=["bo2"], writes=["bo2"])
    for g in range(8):
        S.pool(lambda e, g=g: e.affine_select(out=sgw[:, g, :], in_=sgw[:, g, :], pattern=[[1, 128]], compare_op=ALU.is_ge,
                                              fill=0.0, base=0, channel_multiplier=-1), reads=["sgw"], writes=["sgw"])
    S.dve(TS(dc[:, 0:4], cfm[:, 45:49], 0.5, None, ALU.mult), reads=["cfm"], writes=["dc"])
    S.dve(TS(dc[:, 4:8], cfm[:, 49:53], 0.5, None, ALU.mult), reads=["cfm"], writes=["dc"])
    S.dve(TS(dc[:, 8:12], cfm[:, 57:61], 0.5, None, ALU.mult), reads=["cfm"], writes=["dc"])
    S.dve(TS(dc[:, 12:16], cfm[:, 57:61], -0.5, 1.0, ALU.mult, ALU.add), reads=["cfm"], writes=["dc"])

    lnxg, lnxb = C.cbc[:, 0:512], C.cbc[:, 512:1024]
    sgg, sgb = C.cbc[:, 1024:1536], C.cbc[:, 1536:2048]
    xin = dr["x"]
    steps = [(s, c) for s in range(2) for c in range(NCH)][:DBG_STEPS]
    out_ops = []

    def load_x(i):
        s, c = steps[i]
        S.dma(DMA(C.x[:, i % 2, :], xin[s, c * 128:(c + 1) * 128, :]), writes=[("x", i % 2)], chan=f"xin{i % 2}")

    load_x(0)
    for i, (s, c) in enumerate(steps):
      slot = i % 2
      xs = C.x[:, slot, :]
      kxs = ("x", slot)
      try:
        if i == 0:
            dump(S, "mkn", mkn[:], "mkn", 512)
            dump(S, "mk4", mk4[:], "mk4", 512)
            dump(S, "ones4", ones4[:], "ones4", 512)
        cp(1)
        if i + 1 < len(steps):
            load_x(i + 1)
        if c == 0:
            seq_setup(S, C, s)
            S.pool(MSET(S32[:], 0.0), writes=["S32"])
            S.pool(MSET(SB[:], 0.0), writes=["SB"])
            S.pool(MSET(PAprev[:], 0.0), writes=["PAprev"])
        cp(11)
        norm_and_transpose(S, C, xs, kxs, s)
        hk = [("hT", 0, k) for k in range(8)]
        cp(2)
        if i == 0:
            dump(S, "hT", C.hT[:, 0:2, :].rearrange("p k t -> p (k t)"), hk, 256)
        S.dp(CP(PA[:, :, 0:1], PAprev[:]), reads=["PAprev"], writes=["PAc0"])
        for g0 in range(0, 13, 4):
            nb = min(4, 13 - g0)
            b, kb = bank()
            for q in range(nb):
                blk = g0 + q
                for k in range(8):
                    S.pe(MM(b[:, q * 128:(q + 1) * 128], W[:, k, blk * 128:(blk + 1) * 128], C.hT[:, k, :], k == 0, k == 7),
                         reads=[("W0", k, 0), ("W0", k, 928), ("W0", k, 1856), ("W0", k, 2784), ("hT", 0, k)], writes=[kb])
            src = b[:, 0:nb * 128].rearrange("p (q t) -> p q t", q=nb)
            S.copy(PA[:, g0:g0 + nb, 1:129], src, reads=[kb], writes=[("PA", g0)])
        pak = [("PA", g0) for g0 in range(0, 13, 4)] + ["PAc0"]
        cp(3)
        ub, ukb = bank()
        for k in range(8):
            S.pe(MM(ub[:], C.hT[:, k, :], W[:, k, 1664:2176], k == 0, k == 7), reads=[("W0", k, 0), ("W0", k, 928), ("W0", k, 1856), ("W0", k, 2784), ("hT", 0, k)], writes=[ukb])
        S.copy(ua[:], ub[:], reads=[ukb], writes=["ua"])
        vb, vkb = bank()
        for k in range(8):
            S.pe(MM(vb[:], C.hT[:, k, :], W[:, k, 2176:2688], k == 0, k == 7), reads=[("W0", k, 0), ("W0", k, 928), ("W0", k, 1856), ("W0", k, 2784), ("hT", 0, k)], writes=[vkb])
        group_ln(S, C, vb[:], vkb, vln[:], sgg, sgb, 1e-5, "g")
        S.copy(vlb[:], vln[:], reads=["lnout"], writes=["vlb"])
        for zc in range(2):
            zb, zkb = bank()
            for k in range(8):
                S.pe(MM(zb[:], C.hT[:, k, :], W[:, k, 2688 + zc * 512:3200 + zc * 512], k == 0, k == 7),
                     reads=[("W0", k, 0), ("W0", k, 928), ("W0", k, 1856), ("W0", k, 2784), ("hT", 0, k)], writes=[zkb])
            S.act(ACTF(tzt[:, 0, zc * 512:(zc + 1) * 512], zb[:], AF.Tanh, scale=0.5), reads=[zkb], writes=[("tz", zc)])
            S.dve(STT(zst[:, 1, zc * 512:(zc + 1) * 512], tzt[:, 0, zc * 512:(zc + 1) * 512], 1.0, zb[:], ALU.add, ALU.mult),
                  reads=[zkb, ("tz", zc)], writes=[("zs", zc)])
        mb, mkb = bank()
        for g in range(8):
            S.pe(MM(mb[:, g * 64:(g + 1) * 64], sgw[:, g, :], vlb[:, g * 64:(g + 1) * 64]), reads=["sgw", "vlb"], writes=[mkb])
        S.dve(TT(vln[:].rearrange("p (g d) -> p g d", g=8), mb[:].rearrange("p (g d) -> p g d", g=8),
                 bcl(cfm[:, 65:73], 64), ALU.add), reads=[mkb, "cfm", "vlb"], writes=["lnout"])
        S.dp(TT(vln[:], vln[:], ua[:], ALU.mult), reads=["lnout", "ua"], writes=["lnout"])
        S.dve(STT(C.y[:, 512:1024], vln[:], 0.5, zst[:, 1, 512:1024], ALU.mult, ALU.mult), reads=["lnout", ("zs", 1)], writes=[("y", 0)])
        cp(4)
        S.dp(TT(PM[:], PA[:, :, 0:128], PA[:, :, 1:129], ALU.subtract), reads=pak, writes=["PM"])
        S.dve(TT(PM[:], PM[:], bcl(cfm[:, 32:45], 128), ALU.mult), reads=["PM", "cfm"], writes=["PM"])
        S.dp(TT(PM[:], PM[:], PA[:, :, 1:129], ALU.add), reads=["PM"] + pak, writes=["PM"])
        S.dp(CP(PAprev[:], PA[:, :, 128:129]), reads=pak, writes=["PAprev"])
        r_, k_, v_ = PM[:, 0:4, :], PM[:, 4:8, :], PM[:, 8:12, :]
        S.act(ACTF(TW[0:64, :], PM[0:64, 12, :], AF.Tanh), reads=["PM"], writes=["TW"])
        S.act(ACTF(TW[64:128, :], PM[64:128, 12, :], AF.Copy), reads=["PM"], writes=["TW"])
        wb, wkb = bank()
        ab, akb = bank()
        for j in range(4):
            S.pe(MM(wb[:, j * 128:(j + 1) * 128], lora[0:64, j * 128:(j + 1) * 128], TW[0:64, :]), reads=["lora", "TW"], writes=[wkb])
        for j in range(4):
            S.pe(MM(ab[:, j * 128:(j + 1) * 128], lora[64:128, j * 128:(j + 1) * 128], TW[64:128, :]), reads=["lora", "TW"], writes=[akb])
        for j in range(4):
            S.act(ACTF(th[:, j, :], wb[:, j * 128:(j + 1) * 128], AF.Tanh, bias=dc[:, j:j + 1], scale=0.5),
                  reads=[wkb, "dc"], writes=["th"])
        for j in range(4):
            S.act(ACTF(tha[:, j, :], ab[:, j * 128:(j + 1) * 128], AF.Tanh, bias=dc[:, 4 + j:5 + j], scale=0.5),
                  reads=[akb, "dc"], writes=["tha"])
        thf = th[:].rearrange("p j t -> p (j t)")
        csf = cs[:].rearrange("p j t -> p (j t)")
        S.dve(TS(thf, thf, C1, C1, ALU.mult, ALU.add), reads=["th"], writes=["th"])
        S.dve(lambda e: e.tensor_tensor_scan(out=csf, data0=ones4[:], data1=thf, initial=0.0, op0=ALU.mult, op1=ALU.add),
              reads=["th", "ones4"], writes=["cs"])
        S.act(ACTF(E1[:], cs[:], AF.Exp), reads=["cs"], writes=["E1"])
        S.act(ACTF(E2[:], cs[:], AF.Exp, scale=-1.0), reads=["cs"], writes=["E2"])
        S.dp(TT(th[:], cs[:], th[:], ALU.subtract), reads=["cs", "th"], writes=["th"])
        S.act(ACTF(E3[:], th[:], AF.Exp), reads=["th"], writes=["E3"])
        S.dp(TT(kx[:], k_, bcl(cfm[:, 53:57], 128), ALU.mult), reads=["PM", "cfm"], writes=["kx"])
        S.act(ACTF(sqb[:], kx[:], AF.Square), reads=["kx"], writes=["sqb"])
        sb_, skb = bank()
        for j in range(4):
            S.pe(MM(sb_[:, j * 128:(j + 1) * 128], bones[:], sqb[:, j, :]), reads=["bones", "sqb"], writes=[skb])
        rnf = rn[:].rearrange("p j t -> p (j t)")
        S.dve(TS(rnf, sb_[:], 1e-24, None, ALU.max), reads=[skb], writes=["th"])
        S.act(ACTF(rnf, rnf, AF.Ln), reads=["th"], writes=["th"])
        S.act(ACTF(rnf, rnf, AF.Exp, scale=-0.5), reads=["th"], writes=["th"])
        S.dp(TT(kx[:], kx[:], rn[:], ALU.mult), reads=["kx", "th"], writes=["kx"])
        S.dve(TT(kp[:], tha[:], bcl(dc[:, 8:12], 128), ALU.mult), reads=["tha", "dc"], writes=["kp"])
        S.dp(TT(kp[:], kp[:], bcl(dc[:, 12:16], 128), ALU.add), reads=["kp", "dc"], writes=["kp"])
        S.dp(TT(kp[:], kp[:], k_, ALU.mult), reads=["kp", "PM"], writes=["kp"])
        S.dve(STT(b2[:], tha[:], 1.0, kx[:], ALU.add, ALU.mult), reads=["tha", "kx"], writes=["b2"])
        S.dve(STT(AR[:, :, 0, :], kx[:], -1.0, E3[:], ALU.mult, ALU.mult), reads=["kx", "E3"], writes=["ARa"])
        S.dve(TT(AR[:, :, 1, :], r_, E1[:], ALU.mult), reads=["PM", "E1"], writes=["ARr"])
        S.dve(STT(bt32[:], b2[:], 0.5, E2[:], ALU.mult, ALU.mult), reads=["b2", "E2"], writes=["bt32"])
        S.dp(TT(kt32[:], kp[:], E2[:], ALU.mult), reads=["kp", "E2"], writes=["cs"])
        S.copy(btl[:], bt32[:], reads=["bt32"], writes=["btl"])
        S.copy(ktl[:], kt32[:], reads=["cs"], writes=["ktl"])
        gL = E1[:, :, 127:128]
        S.dp(TT(khb[:], kt32[:], bcl(E1[:, :, 127], 128), ALU.mult), reads=["cs", "E1"], writes=["khb"])
        S.dp(TT(bhb[:], bt32[:], bcl(E1[:, :, 127], 128), ALU.mult), reads=["bt32", "E1"], writes=["bhb"])
        S.dve(TT(b2[:], r_, kp[:], ALU.mult), reads=["PM", "kp", "b2"], writes=["b2"])
        S.dp(TT(rkb[:], b2[:], bcl(cfm[:, 61:65], 128), ALU.mult), reads=["b2", "cfm"], writes=["rkb"])
        if i == 0:
            dump(S, "PM", PM[:].rearrange("p b t -> p (b t)"), "PM", 1664)
            dump(S, "E1", E1[:].rearrange("p b t -> p (b t)"), "E1", 512)
            dump(S, "kk", kx[:].rearrange("p b t -> p (b t)"), "kx", 512)
            dump(S, "kp", kp[:].rearrange("p b t -> p (b t)"), "kp", 512)
            dump(S, "tha", tha[:].rearrange("p b t -> p (b t)"), "tha", 512)
            dump(S, "ua", ua[:], "ua", 512)
            dump(S, "yb", C.y[:, 512:1024], ("y", 0), 512)
        cp(5)
        tbk, tkk = bank()
        tvk = tbk[:].bitcast(BF16)
        for j in range(4):
            S.pe(TR(tvk[:, j * 128:(j + 1) * 128], khb[:, j, :], C.identb[:]), reads=["khb", "identb"], writes=[tkk])
        for j in range(4):
            S.pe(TR(tvk[:, 512 + j * 128:512 + (j + 1) * 128], bhb[:, j, :], C.identb[:]), reads=["bhb", "identb"], writes=[tkk])
        S.copy(kbT[:], tvk, reads=[tkk], writes=["khT", "bhT"])
        vtb, vtk = bank()
        for j in range(4):
            S.pe(TR(vtb[:, j * 128:(j + 1) * 128], PM[:, 8 + j, :], C.ident[:]), reads=["PM", "ident"], writes=[vtk])
        S.copy(vT[:], vtb[:], reads=[vtk], writes=["vT"])
        bob, bok = bank()
        for j in range(4):
            S.pe(MM(bob[:, 2 * j:2 * j + 2], rkb[:, j, :], bo2[:]), reads=["rkb", "bo2"], writes=[bok])
        S.dve(CP(bonus[:], bob[:, 0:8]), reads=[bok], writes=["bonus"])
        cp(6)
        nbs = [bank(), bank()]
        for h in range(8):
            j, r0 = h // 2, (h % 2) * 64
            nb_, nkb = nbs[h % 2]
            S.pe(MM(nb_[:, j * 128:(j + 1) * 128], AR[r0:r0 + 64, j, 0, :], btl[r0:r0 + 64, j, :]),
                 reads=["ARa", "btl"], writes=[nkb])
        for par in range(2):
            nb_, nkb = nbs[par]
            S.dve(TT(Pn[:, 0, :, :].rearrange("p (a two) t -> p a two t", two=2)[:, :, par, :],
                     nb_[:].rearrange("p (h t) -> p h t", h=4),
                     mkn[:].rearrange("p (h t) -> p h t", h=4), ALU.mult), reads=[nkb, "mkn"], writes=[("P", 0, 0), ("P", 0, 1)])
        for h in range(8):
            j, r0 = h // 2, (h % 2) * 64
            b, kb = bank()
            S.pe(MM(b[:, 0:256], btl[r0:r0 + 64, j, :], AR[r0:r0 + 64, j, :, :].rearrange("p a t -> p (a t)")),
                 reads=["btl", "ARa", "ARr"], writes=[kb])
            S.pe(MM(b[:, 256:512], ktl[r0:r0 + 64, j, :], AR[r0:r0 + 64, j, :, :].rearrange("p a t -> p (a t)")),
                 reads=["ktl", "ARa", "ARr"], writes=[kb])
            S.dve(TT(MT[:, h, :], b[:], mk4[:], ALU.mult), reads=[kb, "mk4"], writes=[("MT", h)])
        if i == 0:
            dump(S, "P0h0", Pn[:, 0, 0, :], [("P", 0, 0)], 128)
            dump(S, "P0h1", Pn[:, 0, 1, :], [("P", 0, 0)], 128)
            dump(S, "P0h5", Pn[:, 0, 5, :], [("P", 0, 1)], 128)
        cp(7)
        for g in range(2):
            hs = slice(g * 4, (g + 1) * 4)
            S.dp(CP(PT[:, 0, hs, :], MT[:, hs, 0:128]), reads=[("MT", h) for h in range(g * 4, g * 4 + 4)], writes=[("PT", 0, g)])
            S.dp(TT(TTt[:, hs, :], MT[:, hs, 0:128], bcm(C.identb[:], 4), ALU.add),
                   reads=[("MT", h) for h in range(g * 4, g * 4 + 4)] + ["identb"], writes=[("TT", g)])
        for lv in range(1, 7):
            cur, prv = lv % 2, (lv - 1) % 2
            for g in range(2):
                pb_, pkb = bank()
                for q in range(4):
                    h = g * 4 + q
                    S.pe(MM(pb_[:, q * 128:(q + 1) * 128], PT[:, prv, h, :], Pn[:, prv, h, :]),
                         reads=[("PT", prv, g), ("P", prv, g)], writes=[pkb])
                S.copy(Pn[:, cur, g * 4:(g + 1) * 4, :], pb_[:].rearrange("p (h t) -> p h t", h=4),
                       reads=[pkb], writes=[("P", cur, g)])
                if lv < 6:
                    tb_, tkb = bank()
                    for q in range(4):
                        h = g * 4 + q
                        S.pe(MM(tb_[:, q * 128:(q + 1) * 128], Pn[:, prv, h, :], PT[:, prv, h, :]),
                             reads=[("PT", prv, g), ("P", prv, g)], writes=[tkb])
                    S.copy(PT[:, cur, g * 4:(g + 1) * 4, :], tb_[:].rearrange("p (h t) -> p h t", h=4),
                           reads=[tkb], writes=[("PT", cur, g)])
                db_, dkb = bank()
                for q in range(4):
                    h = g * 4 + q
                    S.pe(MM(db_[:, q * 128:(q + 1) * 128], Pn[:, cur, h, :], TTt[:, h, :]),
                         reads=[("P", cur, g), ("TT", g)], writes=[dkb])
                S.dve(TT(TTt[:, g * 4:(g + 1) * 4, :], db_[:].rearrange("p (h t) -> p h t", h=4), TTt[:, g * 4:(g + 1) * 4, :], ALU.add),
                      reads=[dkb, ("TT", g)], writes=[("TT", g)])
        cp(8)
        mtk = [("MT", h) for h in range(8)]
        rb_, rkb_ = bank()
        for j in range(4):
            S.pe(MM(rb_[:, j * 128:(j + 1) * 128], AR[:, j, 0, :], SB[:, j, :], True, False),
                 reads=["ARa", "SB"], writes=[rkb_])
            for h in (2 * j, 2 * j + 1):
                S.pe(MM(rb_[:, h * 64:(h + 1) * 64], MT[:, h, 256:384], vT[:, h * 64:(h + 1) * 64], False, h % 2 == 1),
                     reads=[("MT", h), "vT"], writes=[rkb_])
        S.copy(Rb[:], rb_[:], reads=[rkb_], writes=["Rb"])
        ub_, ukb_ = bank()
        for h in range(8):
            S.pe(MM(ub_[:, h * 64:(h + 1) * 64], TTt[:, h, :], Rb[:, h * 64:(h + 1) * 64]), reads=[("TT", h // 4), "Rb"], writes=[ukb_])
        S.copy(Ub[:], ub_[:], reads=[ukb_], writes=["Ub"])
        yb_, ykb_ = bank()
        for j in range(4):
            S.pe(MM(yb_[:, j * 128:(j + 1) * 128], AR[:, j, 1, :], SB[:, j, :], True, False),
                 reads=["ARr", "SB"], writes=[ykb_])
            for h in (2 * j, 2 * j + 1):
                S.pe(MM(yb_[:, h * 64:(h + 1) * 64], MT[:, h, 128:256], Ub[:, h * 64:(h + 1) * 64], False, False),
                     reads=[("MT", h), "Ub"], writes=[ykb_])
                S.pe(MM(yb_[:, h * 64:(h + 1) * 64], MT[:, h, 384:512], vT[:, h * 64:(h + 1) * 64], False, h % 2 == 1),
                     reads=[("MT", h), "vT"], writes=[ykb_])
        snb, snk = bank()
        for j in range(4):
            S.pe(MM(snb[:, j * 128:(j + 1) * 128], kbT[:, 512 + j * 128:512 + (j + 1) * 128], Ub[:, j * 128:(j + 1) * 128], True, False),
                 reads=["bhT", "Ub"], writes=[snk])
            S.pe(MM(snb[:, j * 128:(j + 1) * 128], kbT[:, j * 128:(j + 1) * 128], vT[:, j * 128:(j + 1) * 128], False, True),
                 reads=["khT", "vT"], writes=[snk])
        S.dve(TT(Stmp[:], S32[:], bcl(E1[:, :, 127], 64), ALU.mult), reads=["S32", "E1"], writes=["Stmp"])
        sn3 = snb[:].rearrange("p (j c) -> p j c", j=4)
        S.dve(TT(S32[0:64, :, :], Stmp[0:64, :, :], sn3[0:64, :, 0:64], ALU.add), reads=["Stmp", snk, "S32"], writes=["S32"])
        S.dve(TT(S32[64:128, :, :], Stmp[64:128, :, :], sn3[64:128, :, 64:128], ALU.add), reads=["Stmp", snk, "S32"], writes=["S32"])
        S.copy(SB[0:64, :, 0:64], S32[0:64, :, :], reads=["S32"], writes=["SB"])
        S.copy(SB[64:128, :, 64:128], S32[64:128, :, :], reads=["S32"], writes=["SB"])
        cp(9)
        group_ln(S, C, yb_[:], ykb_, ya[:], lnxg, lnxb, 64e-5, "a")
        S.dve(TT(C.tmp[:].rearrange("p (h d) -> p h d", h=8), vT[:].rearrange("p (h d) -> p h d", h=8), bcl(bonus[:], 64), ALU.mult),
              reads=["vT", "bonus"], writes=[("tmp", 0)])
        S.dp(TT(ya[:], ya[:], C.tmp[:], ALU.add), reads=["lnout", ("tmp", 0)], writes=["lnout"])
        S.dve(STT(C.y[:, 0:512], ya[:], 0.5, zst[:, 1, 0:512], ALU.mult, ALU.mult), reads=["lnout", ("zs", 0)], writes=[("y", 0)])
        if i == 0:
            dump(S, "vT", vT[:], "vT", 512)
            dump(S, "bonus", bonus[:], "bonus", 8)
            dump(S, "Ub", Ub[:], "Ub", 512)
            dump(S, "TT0", TTt[:, 0, :], ("TT", 0), 128)
            dump(S, "MT0", MT[:, 0, :], ("MT", 0), 512)
            dump(S, "y", C.y[:], ("y", 0), 1024)
            dump(S, "S32", S32[:].rearrange("p j v -> p (j v)"), "S32", 256)
        cp(10)
        out_proj_residual(S, C, Wo, slot, kxs)
      except Cut:
        pass
      if True:
        out_ops.append(S.dma(DMA(x_out[s, c * 128:(c + 1) * 128, :], xs), reads=[kxs], writes=[("x1", s, c)], chan=f"xo{slot}"))
    return out_ops


def build_A():
    nc = bass.Bass("TRN2", target_bir_lowering=False)
    dr = {}
    dr["x"] = nc.dram_tensor("x", [2, SEQ, D], F32, kind="ExternalInput").ap()
    dr["cT"] = nc.dram_tensor("cT", [128, 8, 2], F32, kind="ExternalInput").ap()
    dr["ada_w"] = nc.dram_tensor("ada_w0", [D, 3 * D], F32, kind="ExternalInput").ap()
    dr["e_w_in"] = nc.dram_tensor("e_w_in", [D, 3712], F32, kind="ExternalInput").ap()
    dr["e_w_out"] = nc.dram_tensor("e_w_out", [D, D], F32, kind="ExternalInput").ap()
    dr["cfm"] = nc.dram_tensor("cfm", [128, NC_FM], F32, kind="ExternalInput").ap()
    dr["cbc"] = nc.dram_tensor("cbcA", [128, 2048], F32, kind="ExternalInput").ap()
    dr["lora"] = nc.dram_tensor("lora", [128, 512], F32, kind="ExternalInput").ap()
    dr["sgwT"] = nc.dram_tensor("sgwT", [128, 8, 128], F32, kind="ExternalInput").ap()
    x1 = nc.dram_tensor("x1", [2, SEQ, D], F32, kind="ExternalOutput").ap()
    DUMP["ops"] = []
    if DUMP["on"]:
        DUMP["ap"] = nc.dram_tensor("dbg", [128, 16384], F32, kind="ExternalOutput").ap()
        DUMP["col"] = 0
    S = Sched(nc)
    with contextlib.ExitStack() as st:
        outs = phaseA(nc, S, st, dr, x1)
        S.emit(final_wait_ops=outs[-2:] + DUMP["ops"])
    return nc


def fm(v, nblk):
    return np.ascontiguousarray(np.asarray(v, np.float32).reshape(nblk, 128).T)


def host_consts(inp):
    cfm = np.zeros((128, NC_FM), np.float32)
    cfm[:, 0:8] = fm(inp["norm_g"][0], 8)
    cfm[:, 8:32] = fm(inp["ada_b"][0], 24)
    cfm[:, 32:45] = fm(inp["e_mu"][0], 13)
    cfm[:, 45:49] = fm(inp["e_w0"][0], 4)
    cfm[:, 49:53] = fm(inp["e_a0"][0], 4)
    cfm[:, 53:57] = fm(inp["e_k_k"][0], 4)
    cfm[:, 57:61] = fm(inp["e_k_a"][0], 4)
    cfm[:, 61:65] = fm(inp["e_r_k"][0].reshape(-1), 4)
    cfm[:, 65:73] = np.asarray(inp["e_sg_b"][0], np.float32).T
    cfm[:, 73:81] = fm(inp["norm_g"][1], 8)
    cfm[:, 81:105] = fm(inp["ada_b"][1], 24)
    row = np.concatenate([inp["e_lnx_g"][0], inp["e_lnx_b"][0], inp["e_sg_ln_g"][0], inp["e_sg_ln_b"][0],
                          inp["final_g"], inp["o_sinks"][0]]).astype(np.float32)
    cbc = np.ascontiguousarray(np.broadcast_to(row[None, :], (128, row.shape[0])))
    lora = np.ascontiguousarray(np.concatenate([inp["e_w_up"][0], inp["e_a_up"][0]], axis=0).astype(np.float32))
    sgwT = np.ascontiguousarray(np.transpose(np.asarray(inp["e_sg_w"][0], np.float32), (2, 0, 1)))
    return cfm, cbc, lora, sgwT


_CACHE = {}


def run_A(inp):
    if "A" not in _CACHE:
        _CACHE["A"] = build_A()
    nc = _CACHE["A"]
    cfm, cbc, lora, sgwT = host_consts(inp)
    x = np.asarray(inp["x"], np.float32)
    c = np.asarray(inp["c"], np.float32)
    maps = []
    for i in range(NCORES):
        cs = c[2 * i:2 * i + 2]
        cT = np.ascontiguousarray(cs.reshape(2, 8, 128).transpose(2, 1, 0))
        maps.append({"x": np.ascontiguousarray(x[2 * i:2 * i + 2]), "cT": cT,
                     "ada_w0": np.ascontiguousarray(inp["ada_w"][0]), "e_w_in": np.ascontiguousarray(inp["e_w_in"][0]),
                     "e_w_out": np.ascontiguousarray(inp["e_w_out"][0]), "cfm": cfm,
                     "cbcA": np.ascontiguousarray(cbc[:, 0:2048]), "lora": lora, "sgwT": sgwT})
    res = run_bass_kernel_spmd(nc, maps, core_ids=list(range(NCORES)))
    if DUMP["on"]:
        DUMP["data"] = res.results[0]["dbg"]
    return np.concatenate([r["x1"] for r in res.results], axis=0)


SLOPES = [2.0 ** (-8.0 * (h + 1) / 16.0) for h in range(16)]
W1C = 2432


def phaseB(nc, S, st, dr, x_out):
    C = Ctx()
    sb = lambda name, shape, dt=F32: st.enter_context(nc.sbuf_tensor("sb_b" + name, shape, dt))
    NB = 2
    common_setup(nc, S, st, C, dr, 1, 1040, "b", nbuf=NB, nx=3)
    bank = C.bank
    W = sb("Win1", [128, 8, W1C], BF16)
    Wo = sb("Wout1", [128, 8, 1024], BF16)
    S.filler = MM(C.warm[:], C.identb[:], W[:, 0, 0:512])
    E = sb("Ealibi", [128, 2, 16, 128], BF16)
    PTs = [sb(f"PTatt{i}", [128, 2, 16, 128], BF16) for i in range(NB)]
    ex = sb("expt", [128, 4, 512])
    qTs = [sb(f"qT{i}", [128, 8, 128], BF16) for i in range(NB)]
    kT2 = sb("kT2", [128, 2, 3, 128], BF16)
    V1 = sb("V1", [128, 3, 2, 65], BF16)
    esink = sb("esink", [128, 16])
    Dm = sb("Dm", [128, 128])
    qrow = sb("qrow", [128, 128])
    kcol = sb("kcol", [128, 1])
    dens = [sb(f"den{i}", [128, 16]) for i in range(NB)]
    atts = [sb(f"att{i}", [128, 1024]) for i in range(NB)]
    obs = [sb(f"ob{i}", [128, 1024]) for i in range(NB)]
    fg = C.cbc[:, 0:1024]
    C.negs = sb("negs", [128, 16])
    for h in range(16):
        S.pool(MSET(C.negs[:, h:h + 1], -128.0 * SLOPES[h]), writes=["negs"])

    nq = 0
    for k in range(8):
        for c0 in range(0, W1C, 608):
            S.dma(DMA(W[:, k, c0:c0 + 608], dr["o_w_in"][k * 128:(k + 1) * 128, c0:c0 + 608]),
                  writes=[("W1", k, c0), ("wqchain", nq % 4)], chan=f"wq{nq % 4}", eng="pool")
            nq += 1
    for k in range(8):
        S.dma(DMA(Wo[:, k, :], dr["o_w_out"][k * 128:(k + 1) * 128, :]), writes=[("W", k), ("wqchain", nq % 4)], chan=f"wq{nq % 4}", eng="pool")
        nq += 1
    wk = lambda k: [("W1", k, c0) for c0 in range(0, W1C, 608)]
    S.dve(lambda e: e.tensor_tensor_scan(out=qrow[:], data0=C.ones[:], data1=C.ones[:], initial=-1.0, op0=ALU.mult, op1=ALU.add),
          reads=["ones"], writes=["qrow"])
    S.dve(TT(Dm[:], qrow[:], C.ident[:], ALU.mult), reads=["qrow", "ident"], writes=["Dm"])
    S.dve(RED(kcol[:], Dm[:]), reads=["Dm"], writes=["kcol"])
    S.dve(TS(Dm[:], qrow[:], kcol[:, 0:1], None, ALU.subtract), reads=["qrow", "kcol"], writes=["Dm"])
    for h in range(16):
        sl = h % 4
        S.act(ACTF(ex[:, sl, 0:128], Dm[:], AF.Exp, scale=-SLOPES[h]), reads=["Dm"], writes=[("ex", sl)])
        S.pool(lambda e, h=h, sl=sl: e.affine_select(out=E[:, 1, h, :], in_=ex[:, sl, 0:128], pattern=[[1, 128]], compare_op=ALU.is_ge,
                                                     fill=0.0, base=0, channel_multiplier=-1), reads=[("ex", sl)], writes=[("E", h, 1)])
        S.act(ACTF(ex[:, sl, 128:256], Dm[:], AF.Exp, scale=-SLOPES[h], bias=C.negs[:, h:h + 1]), reads=["Dm", "negs"], writes=[("ex2", sl)])
        S.pool(lambda e, h=h, sl=sl: e.affine_select(out=E[:, 0, h, :], in_=ex[:, sl, 128:256], pattern=[[-1, 128]], compare_op=ALU.is_gt,
                                                     fill=0.0, base=0, channel_multiplier=1), reads=[("ex2", sl)], writes=[("E", h, 0)])
    Ek = [("E", h, pc) for h in range(16) for pc in range(2)]
    S.act(ACTF(esink[:], C.cbc[:, 1024:1040], AF.Exp), reads=["cbc"], writes=["esink"])
    S.pool(MSET(V1[:], 1.0), writes=[("V1", r) for r in range(3)])

    xin = dr["x1"]
    steps = [(s, c) for s in range(2) for c in range(NCH)][:DBG_STEPS]
    out_ops = []
    nx = C.nx

    def load_x(i):
        s, c = steps[i]
        S.dma(DMA(C.x[:, i % nx, :], xin[s, c * 128:(c + 1) * 128, :]), reads=[("x1", s, c)], writes=[("x", i % nx)], chan=f"xin{i % nx}")

    load_x(0)
    n_ex = 0
    for i, (s, c) in enumerate(steps):
        slot = i % nx
        par = i % NB
        xs = C.x[:, slot, :]
        kxs = ("x", slot)
        hT, qT, PT, den, att, ob = C.hTs[par], qTs[par], PTs[par], dens[par], atts[par], obs[par]
        tzt = zst = C.stages[par]
        y = C.ys[par]
        if i + 1 < len(steps):
            load_x(i + 1)
        if c == 0:
            seq_setup(S, C, s)
        norm_and_transpose(S, C, xs, kxs, s, par)
        cur, prv = c % 3, (c - 1) % 3
        for g0 in (0, 4, 8):
            nb = 4 if g0 < 8 else 2
            b, kb = bank()
            for q in range(nb):
                blk = g0 + q
                for k in range(8):
                    S.pe(MM(b[:, q * 128:(q + 1) * 128], W[:, k, blk * 128:(blk + 1) * 128], hT[:, k, :], k == 0, k == 7),
                         reads=wk(k) + [("hT", par, k)], writes=[kb])
            src = b[:, 0:nb * 128].rearrange("p (q t) -> p q t", q=nb)
            if g0 < 8:
                S.act(ACTF(qT[:, g0:g0 + 4, :], src, AF.Copy, scale=0.125), reads=[kb], writes=[("qT", par, g0)])
            else:
                S.copy(kT2[:, :, cur, :], src, reads=[kb], writes=[("kT", cur)])
        vb, vkb = bank()
        for k in range(8):
            S.pe(MM(vb[:, 0:128], hT[:, k, :], W[:, k, 1280:1408], k == 0, k == 7), reads=wk(k) + [("hT", par, k)], writes=[vkb])
        S.copy(V1[:, cur, :, 0:64], vb[:, 0:128].rearrange("p (g d) -> p g d", g=2), reads=[vkb], writes=[("V1", cur)])
        for zc in range(2):
            zb, zkb = bank()
            for k in range(8):
                S.pe(MM(zb[:], hT[:, k, :], W[:, k, 1408 + zc * 512:1920 + zc * 512], k == 0, k == 7),
                     reads=wk(k) + [("hT", par, k)], writes=[zkb])
            S.act(ACTF(tzt[:, 0, zc * 512:(zc + 1) * 512], zb[:], AF.Tanh, scale=0.5), reads=[zkb], writes=[("tz", par, zc)])
            S.dve(STT(zst[:, 1, zc * 512:(zc + 1) * 512], tzt[:, 0, zc * 512:(zc + 1) * 512], 1.0, zb[:], ALU.add, ALU.mult),
                  reads=[zkb, ("tz", par, zc)], writes=[("zs", par, zc)])
        E4 = lambda pc: E[:, pc, :, :].rearrange("p (a two) q -> p a two q", two=2)
        P4 = lambda pc: PT[:, pc, :, :].rearrange("p (a two) q -> p a two q", two=2)
        for g in range(2):
            for pc in ((1, 0) if c > 0 else (1,)):
                ring = cur if pc == 1 else prv
                for pr in range(2):
                    r0 = 64 * pr
                    blk = 0 if g == pr else 1
                    b, kb = bank()
                    for a in range(4):
                        j = 4 * g + a
                        S.pe(MM(b[:, a * 128:(a + 1) * 128], kT2[r0:r0 + 64, blk, ring, :], qT[r0:r0 + 64, j, :]),
                             reads=[("kT", ring), ("qT", par, 0), ("qT", par, 4)], writes=[kb])
                    sl = n_ex % 4
                    n_ex += 1
                    S.act(ACTF(ex[:, sl, :], b[:], AF.Exp), reads=[kb], writes=[("ex", sl), ("ex2", sl)])
                    S.dve(TT(P4(pc)[:, 4 * g:4 * g + 4, pr, :], ex[:, sl, :].rearrange("p (a q) -> p a q", a=4),
                             E4(pc)[:, 4 * g:4 * g + 4, pr, :], ALU.mult), reads=[("ex", sl), ("ex2", sl)] + Ek, writes=[("PT", par, g, pc, pr)])
        for hb in range(4):
            b, kb = bank()
            for a in range(4):
                h = hb * 4 + a
                g = h // 8
                pcs = (1, 0) if c > 0 else (1,)
                for n_, pc in enumerate(pcs):
                    ring = cur if pc == 1 else prv
                    S.pe(MM(b[:, a * 65:(a + 1) * 65], PT[:, pc, h, :], V1[:, ring, g, :], n_ == 0, n_ == len(pcs) - 1),
                         reads=[("PT", par, g, pc, h % 2), ("V1", ring)], writes=[kb])
            b3 = b[:, 0:260].rearrange("p (a e) -> p a e", a=4)
            S.dve(TT(den[:, hb * 4:hb * 4 + 4], b3[:, :, 64], esink[:, hb * 4:hb * 4 + 4], ALU.add), reads=[kb, "esink"], writes=[("den", par, hb)])
            S.dve(_wc(lambda e, hb=hb, den=den: e.reciprocal(out=den[:, hb * 4:hb * 4 + 4], in_=den[:, hb * 4:hb * 4 + 4]), 0.15),
                  reads=[("den", par, hb)], writes=[("den", par, hb)])
            S.dve(TT(att[:, hb * 256:(hb + 1) * 256].rearrange("p (a d) -> p a d", a=4), b3[:, :, 0:64],
                     bcl(den[:, hb * 4:hb * 4 + 4], 64), ALU.mult), reads=[kb, ("den", par, hb)], writes=[("att", par, hb)])
        for zc in range(2):
            S.dp(TT(att[:, zc * 512:(zc + 1) * 512], att[:, zc * 512:(zc + 1) * 512], zst[:, 1, zc * 512:(zc + 1) * 512], ALU.mult),
                   reads=[("att", par, 2 * zc), ("att", par, 2 * zc + 1), ("zs", par, zc)], writes=[("att", par, 2 * zc), ("att", par, 2 * zc + 1)])
            S.act(ACTF(y[:, zc * 512:(zc + 1) * 512], att[:, zc * 512:(zc + 1) * 512], AF.Copy, scale=0.5),
                  reads=[("att", par, 2 * zc), ("att", par, 2 * zc + 1)], writes=[("y", par)])
        out_proj_residual(S, C, Wo, slot, kxs, par)
        xn, sm = C.xns[par], C.sms[par]
        S.act(ACTF(att[:], xs, AF.Square), reads=[kxs] + [("att", par, q) for q in range(4)], writes=[("att", par, q) for q in range(4)])
        S.dve(RED(sm[:, 4:5], att[:]), reads=[("att", par, q) for q in range(4)], writes=[("sm4", par)])
        S.dve(TS(sm[:, 5:6], sm[:, 4:5], 1.0 / D, 1e-6, ALU.mult, ALU.add), reads=[("sm4", par)], writes=[("sm5", par)])
        S.pool(TT(sm[:, 6:7], sm[:, 5:6], C.mhalf[:, 0:1], ALU.pow), reads=[("sm5", par), "mhalf"], writes=[("sm6", par)])
        S.dve(STT(ob[:], xs, sm[:, 6:7], fg, ALU.mult, ALU.mult), reads=[kxs, ("sm6", par), "cbc"], writes=[("ob", par)])
        out_ops.append(S.dma(DMA(x_out[s, c * 128:(c + 1) * 128, :], ob[:]), reads=[("ob", par)], writes=[("out", s, c)], chan=f"oo{par}"))
    return out_ops


def build_B():
    nc = bass.Bass("TRN2", target_bir_lowering=False)
    dr = {}
    dr["x1"] = nc.dram_tensor("x1", [2, SEQ, D], F32, kind="ExternalInput").ap()
    dr["cT"] = nc.dram_tensor("cT", [128, 8, 2], F32, kind="ExternalInput").ap()
    dr["ada_w"] = nc.dram_tensor("ada_w1", [D, 3 * D], F32, kind="ExternalInput").ap()
    dr["o_w_in"] = nc.dram_tensor("o_w_in", [D, W1C], F32, kind="ExternalInput").ap()
    dr["o_w_out"] = nc.dram_tensor("o_w_out", [D, D], F32, kind="ExternalInput").ap()
    dr["cfm"] = nc.dram_tensor("cfm", [128, NC_FM], F32, kind="ExternalInput").ap()
    dr["cbc"] = nc.dram_tensor("cbcB", [128, 1040], F32, kind="ExternalInput").ap()
    out = nc.dram_tensor("out", [2, SEQ, D], F32, kind="ExternalOutput").ap()
    S = Sched(nc)
    with contextlib.ExitStack() as st:
        outs = phaseB(nc, S, st, dr, out)
        S.emit(final_wait_ops=outs[-2:])
    return nc


def host_w1(inp):
    w = np.asarray(inp["o_w_in"][0], np.float32)
    q, k, v, z = w[:, 0:1024], w[:, 1024:1152], w[:, 1152:1280], w[:, 1280:2304]
    ksw = np.concatenate([k[:, 64:128], k[:, 0:64]], axis=1)
    return np.ascontiguousarray(np.concatenate([q, k, ksw, v, z], axis=1))


def run_B(inp, x1):
    if "B" not in _CACHE:
        _CACHE["B"] = build_B()
    nc = _CACHE["B"]
    cfm, cbc, lora, sgwT = host_consts(inp)
    c = np.asarray(inp["c"], np.float32)
    w1 = host_w1(inp)
    maps = []
    for i in range(NCORES):
        cs = c[2 * i:2 * i + 2]
        cT = np.ascontiguousarray(cs.reshape(2, 8, 128).transpose(2, 1, 0))
        maps.append({"x1": np.ascontiguousarray(x1[2 * i:2 * i + 2]), "cT": cT,
                     "ada_w1": np.ascontiguousarray(inp["ada_w"][1]), "o_w_in": w1,
                     "o_w_out": np.ascontiguousarray(inp["o_w_out"][0]), "cfm": cfm,
                     "cbcB": np.ascontiguousarray(cbc[:, 2048:3088])})
    res = run_bass_kernel_spmd(nc, maps, core_ids=list(range(NCORES)))
    return np.concatenate([r["out"] for r in res.results], axis=0)


def build_fused(plan=True):
    if plan and not BANK_ASSIGN:
        _plan_pass()
    return _build_fused_real()


def _plan_pass():
    nc = bass.Bass("TRN2", target_bir_lowering=False)
    inp = lambda n, shp: nc.dram_tensor(n, shp, F32, kind="ExternalInput").ap()
    drA = {"x": inp("x", [2, SEQ, D]), "cT": inp("cT", [128, 8, 2]), "ada_w": inp("ada_w0", [D, 3 * D]),
           "e_w_in": inp("e_w_in", [D, 3712]), "e_w_out": inp("e_w_out", [D, D]), "cfm": inp("cfm", [128, NC_FM]),
           "cbc": inp("cbcA", [128, 2048]), "lora": inp("lora", [128, 512]), "sgwT": inp("sgwT", [128, 8, 128])}
    x1 = nc.dram_tensor("x1", [2, SEQ, D], F32, kind="Internal").ap()
    drB = {"x1": x1, "cT": drA["cT"], "ada_w": inp("ada_w1", [D, 3 * D]), "o_w_in": inp("o_w_in", [D, W1C]),
           "o_w_out": inp("o_w_out", [D, D]), "cfm": drA["cfm"], "cbc": inp("cbcB", [128, 1040])}
    out = nc.dram_tensor("out", [2, SEQ, D], F32, kind="ExternalOutput").ap()
    DUMP["ops"] = []
    S = Sched(nc)
    S.plan = []
    with contextlib.ExitStack() as st:
        phaseA(nc, S, st, drA, x1)
        S._flush()
    with contextlib.ExitStack() as st:
        phaseB(nc, S, st, drB, out)
        S._flush()
    BANK_ASSIGN["a"], BANK_ASSIGN["b"] = S.plan[0], S.plan[1]


def _build_fused_real():
    nc = bass.Bass("TRN2", target_bir_lowering=False)
    inp = lambda n, shp: nc.dram_tensor(n, shp, F32, kind="ExternalInput").ap()
    drA = {"x": inp("x", [2, SEQ, D]), "cT": inp("cT", [128, 8, 2]), "ada_w": inp("ada_w0", [D, 3 * D]),
           "e_w_in": inp("e_w_in", [D, 3712]), "e_w_out": inp("e_w_out", [D, D]), "cfm": inp("cfm", [128, NC_FM]),
           "cbc": inp("cbcA", [128, 2048]), "lora": inp("lora", [128, 512]), "sgwT": inp("sgwT", [128, 8, 128])}
    x1 = nc.dram_tensor("x1", [2, SEQ, D], F32, kind="Internal").ap()
    drB = {"x1": x1, "cT": drA["cT"], "ada_w": inp("ada_w1", [D, 3 * D]), "o_w_in": inp("o_w_in", [D, W1C]),
           "o_w_out": inp("o_w_out", [D, D]), "cfm": drA["cfm"], "cbc": inp("cbcB", [128, 1040])}
    out = nc.dram_tensor("out", [2, SEQ, D], F32, kind="ExternalOutput").ap()
    DUMP["ops"] = []
    S = Sched(nc)
    if "a" in BANK_ASSIGN:
        S.bank_plans = [BANK_ASSIGN["a"], BANK_ASSIGN["b"]]
    with contextlib.ExitStack() as sems:
        with contextlib.ExitStack() as st:
            phaseA(nc, S, st, drA, x1)
            S.barrier()
            S.emit(sem_stack=sems)
        with contextlib.ExitStack() as st:
            outs = phaseB(nc, S, st, drB, out)
            S.emit(final_wait_ops=outs[-2:], sem_stack=sems)
    return nc


def kernel(**inputs):
    inp = inputs
    if "F" not in _CACHE:
        _CACHE["F"] = build_fused()
    nc = _CACHE["F"]
    cfm, cbc, lora, sgwT = host_consts(inp)
    x = np.asarray(inp["x"], np.float32)
    c = np.asarray(inp["c"], np.float32)
    w1 = host_w1(inp)
    shared = {"ada_w0": np.ascontiguousarray(inp["ada_w"][0]), "e_w_in": np.ascontiguousarray(inp["e_w_in"][0]),
              "e_w_out": np.ascontiguousarray(inp["e_w_out"][0]), "cfm": cfm, "cbcA": np.ascontiguousarray(cbc[:, 0:2048]),
              "lora": lora, "sgwT": sgwT, "ada_w1": np.ascontiguousarray(inp["ada_w"][1]), "o_w_in": w1,
              "o_w_out": np.ascontiguousarray(inp["o_w_out"][0]), "cbcB": np.ascontiguousarray(cbc[:, 2048:3088])}
    maps = []
    for i in range(NCORES):
        cs = c[2 * i:2 * i + 2]
        cT = np.ascontiguousarray(cs.reshape(2, 8, 128).transpose(2, 1, 0))
        m = dict(shared)
        m["x"] = np.ascontiguousarray(x[2 * i:2 * i + 2])
        m["cT"] = cT
        maps.append(m)
    res = run_bass_kernel_spmd(nc, maps, core_ids=list(range(NCORES)))
    return np.concatenate([r["out"] for r in res.results], axis=0).astype(np.float32)
```

```python
import contextlib
import numpy as np
import concourse.bass as bass
import concourse.mybir as mybir
from concourse.bass_utils import run_bass_kernel_spmd

F32 = mybir.dt.float32
BF16 = mybir.dt.bfloat16
AF = mybir.ActivationFunctionType
ALU = mybir.AluOpType
AX = mybir.AxisListType

ENGS = ("pe", "act", "dve", "pool", "sp")
NCORES = 8
SEQ = 2048
D = 1024
NCH = SEQ // 128
NC_FM = 105
C1 = -0.5 * float(np.exp(-0.5))
DBG_STEPS = 32
DBG_CUT = 99
import os as _os
EVM = 0


class Cut(Exception):
    pass


DUMP = {"on": False, "col": 0, "ap": None, "names": {}}


def dump(S, name, ap, key, n, bf=False):
    if not DUMP["on"]:
        return
    c0 = DUMP["col"]
    DUMP["names"][name] = (c0, n)
    DUMP["col"] += n
    rid = S.dma(DMA(DUMP["ap"][:, c0:c0 + n], ap), reads=[key] if not isinstance(key, list) else key, writes=[("dbg", name)],
                chan="dbg_" + name, eng="pool")
    DUMP["ops"].append(rid)


def cp(n):
    if n == DBG_CUT:
        raise Cut()


class Sched:
    LAT = 0.25

    def __init__(self, nc):
        self.nc = nc
        self.ops = []
        self.rec = []
        self.seen = {e: {} for e in ENGS}
        self.domain_ops = {}
        self.rec2op = {}
        self.nrec = 0
        self.reorder = True
        self.plan = None
        self.bank_plans = None
        self.seg = 0

    def add(self, eng, fn, reads=(), writes=(), chan=None, extra=()):
        rid = self.nrec
        self.nrec += 1
        cost = getattr(fn, "cost", None)
        if cost is None:
            cost = {"pe": 0.1, "act": 0.5, "dve": 0.5, "pool": 0.8, "sp": 2.0}[eng]
        if eng == "pool" and chan is None:
            cost = cost * 2.0
        if chan is not None:
            cost = max(cost, 2.0)
        self.rec.append(dict(rid=rid, eng=eng, fn=fn, reads=list(reads), writes=list(writes), chan=chan, cost=cost))
        return rid

    def pe(self, fn, reads=(), writes=()):
        return self.add("pe", fn, reads, writes)

    def act(self, fn, reads=(), writes=()):
        return self.add("act", fn, reads, writes)

    def dve(self, fn, reads=(), writes=()):
        return self.add("dve", fn, reads, writes)

    def pool(self, fn, reads=(), writes=()):
        return self.add("pool", fn, reads, writes)

    def dma(self, fn, reads=(), writes=(), chan="c0", eng="sp"):
        return self.add(eng, fn, reads, writes, chan=chan)

    def dp(self, fn, reads=(), writes=()):
        rid = self.add("pool", fn, reads, writes)
        r = self.rec[-1]
        r["alts"] = {"pool": (fn, r["cost"]), "dve": (fn, r["cost"] / 2.0)}
        return rid

    def copy(self, out, in_, reads=(), writes=()):
        fa, fd = ACTF(out, in_, AF.Copy), CP(out, in_)
        rid = self.add("act", fa, reads, writes)
        self.rec[-1]["alts"] = {"act": (fa, fa.cost), "dve": (fd, fd.cost)}
        return rid

    def _flush(self):
        import heapq
        recs = self.rec
        self.rec = []
        n = len(recs)
        if n == 0:
            return
        lastw, readers = {}, {}
        preds = [set() for _ in range(n)]
        for i, r in enumerate(recs):
            reads, writes = r["reads"], r["writes"]
            excl = [k for k in reads if isinstance(k, tuple) and k[0] in ("pb", "pbj")]
            if excl:
                reads = [k for k in reads if k not in excl]
                writes = writes + excl
            d = preds[i]
            for k in reads:
                if k in lastw:
                    d.add(lastw[k])
            for k in writes:
                if RELAX and (k if isinstance(k, str) else k[0]) in RELAX:
                    continue
                if k in lastw:
                    d.add(lastw[k])
                d.update(readers.get(k, ()))
            d.discard(i)
            for k in reads:
                readers.setdefault(k, []).append(i)
            for k in writes:
                lastw[k] = i
                readers[k] = []
        order = list(range(n))
        if self.plan is None and self.bank_plans:
            assign, acq = self.bank_plans[self.seg]
            self.seg += 1
            self._chain_banks(recs, preds, assign, acq)
        if self.plan is not None:
            self.plan.append(self._plan_banks(recs, preds))
            return
        if self.reorder:
            order, mk = self._listsched(recs, preds)
            self.last_makespan = mk
            print("[sched] segment ops=%d simulated makespan=%.1f us" % (n, mk))
        pos_of = {}
        for i in order:
            r = recs[i]
            idx = self._commit(r, [pos_of[p] for p in preds[i]])
            pos_of[i] = idx
            self.rec2op[r["rid"]] = idx

    @staticmethod
    def _chain_banks(recs, preds, assign, acq):
        first, last = {}, {}
        for i, r in enumerate(recs):
            for k in r["reads"] + r["writes"]:
                if isinstance(k, tuple) and k[0] == "pbj":
                    first.setdefault(k[1], i)
                    last[k[1]] = i
        prev_on = {}
        for j in acq:
            b = assign[j]
            if b in prev_on:
                preds[first[j]].add(last[prev_on[b]])
            prev_on[b] = j

    def _listsched(self, recs, preds):
        import heapq
        n = len(recs)
        succs = [[] for _ in range(n)]
        for i in range(n):
            for p in preds[i]:
                succs[p].append(i)
        prio = [0.0] * n
        for i in range(n - 1, -1, -1):
            m = 0.0
            for q in succs[i]:
                if prio[q] > m:
                    m = prio[q]
            prio[i] = recs[i]["cost"] + self.LAT + m
        indeg = [len(preds[i]) for i in range(n)]
        ready_t = [0.0] * n
        fin = [0.0] * n
        heaps = {e: [] for e in ENGS}

        def push(i):
            alts = recs[i].get("alts")
            for e_ in (alts if alts else (recs[i]["eng"],)):
                heapq.heappush(heaps[e_], (-prio[i], i))
        for i in range(n):
            if indeg[i] == 0:
                push(i)
        free = {e: 0.0 for e in ENGS}
        start = [0.0] * n
        done = 0
        while done < n:
            best = None
            for e in ENGS:
                h = heaps[e]
                if not h:
                    continue
                cand = heapq.nsmallest(LS_K, h)
                tb, cb = None, None
                for c in cand:
                    t = max(free[e], ready_t[c[1]])
                    if tb is None or t < tb - LS_EPS:
                        tb, cb = t, c
                if best is None or tb < best[0] - 1e-9:
                    best = (tb, e, cb)
            t, e, c = best
            i = c[1]
            r = recs[i]
            for e_ in (r["alts"] if r.get("alts") else (e,)):
                heaps[e_].remove(c)
                heapq.heapify(heaps[e_])
            if r.get("alts"):
                r["eng"] = e
                r["fn"], r["cost"] = r["alts"][e]
            start[i] = t
            if r["chan"] is not None:
                free[e] = t + 0.1
                fin[i] = t + r["cost"]
            else:
                free[e] = t + r["cost"]
                fin[i] = free[e]
            done += 1
            for q in succs[i]:
                rt = fin[i] + (self.LAT if recs[q]["eng"] != e or r["chan"] is not None else (0.0 if e == "pe" else 0.05))
                if rt > ready_t[q]:
                    ready_t[q] = rt
                indeg[q] -= 1
                if indeg[q] == 0:
                    push(q)
        order = sorted(range(n), key=lambda i: (start[i], i))
        return order, max(fin)

    def _plan_banks(self, recs, preds, nbanks=8):
        import heapq
        n = len(recs)
        jobs_of = []
        job_ops = {}
        for i, r in enumerate(recs):
            js = sorted({k[1] for k in r["reads"] + r["writes"] if isinstance(k, tuple) and k[0] == "pbj"})
            jobs_of.append(js)
            for j in js:
                job_ops[j] = job_ops.get(j, 0) + 1
        succs = [[] for _ in range(n)]
        for i in range(n):
            for p in preds[i]:
                succs[p].append(i)
        prio = [0.0] * n
        for i in range(n - 1, -1, -1):
            m = 0.0
            for q in succs[i]:
                if prio[q] > m:
                    m = prio[q]
            prio[i] = recs[i]["cost"] + self.LAT + m
        def run(nbanks, all_jobs, inorder=True):
            starts = []
            indeg = [len(preds[i]) for i in range(n)]
            ready_t = [0.0] * n
            fin = [0.0] * n
            heaps = {e: [] for e in ENGS}
            for i in range(n):
                if indeg[i] == 0:
                    heapq.heappush(heaps[recs[i]["eng"]], (-prio[i], i))
            free = {e: 0.0 for e in ENGS}
            bank_free = [0.0] * min(nbanks, 4096)
            INF = float("inf")
            assign = {}
            job_left = dict(job_ops)
            job_fin = {}
            nxt = 0
            done = 0
            while done < n:
                best = None
                for e in ENGS:
                    h = heaps[e]
                    if not h:
                        continue
                    cand = heapq.nsmallest(8, h)
                    tb, cb = None, None
                    for c in cand:
                        i = c[1]
                        t = max(free[e], ready_t[i])
                        new = [j for j in jobs_of[i] if j not in assign]
                        if new:
                            if inorder and (len(new) > 1 or nxt >= len(all_jobs) or new[0] != all_jobs[nxt]):
                                continue
                            bf = min(bank_free)
                            if bf == INF:
                                continue
                            t = max(t, bf)
                        if tb is None or t < tb - 1e-9:
                            tb, cb = t, c
                    if cb is not None and (best is None or tb < best[0] - 1e-9):
                        best = (tb, e, cb)
                if best is None:
                    for e in ENGS:
                        for c in heaps[e]:
                            i = c[1]
                            new = [j for j in jobs_of[i] if j not in assign]
                            if (not new or (len(new) == 1 and nxt < len(all_jobs) and new[0] == all_jobs[nxt])) and \
                                    (not new or min(bank_free) < INF):
                                t = max(free[e], ready_t[i], min(bank_free) if new else 0.0)
                                if best is None or t < best[0]:
                                    best = (t, e, c)
                    assert best is not None, "bank planning deadlock"
                t, e, c = best
                heaps[e].remove(c)
                heapq.heapify(heaps[e])
                i = c[1]
                r = recs[i]
                for j in jobs_of[i]:
                    if j not in assign:
                        b = min(range(len(bank_free)), key=lambda x: bank_free[x])
                        assign[j] = b
                        bank_free[b] = INF
                        starts.append(j)
                        nxt += 1
                if r["chan"] is not None:
                    free[e] = t + 0.1
                    fin[i] = t + r["cost"]
                else:
                    free[e] = t + r["cost"]
                    fin[i] = free[e]
                for j in jobs_of[i]:
                    job_fin[j] = max(job_fin.get(j, 0.0), fin[i])
                    job_left[j] -= 1
                    if job_left[j] == 0:
                        bank_free[assign[j]] = job_fin[j] + self.LAT
                done += 1
                for q in succs[i]:
                    rt = fin[i] + (self.LAT if recs[q]["eng"] != e or r["chan"] is not None else 0.05)
                    if rt > ready_t[q]:
                        ready_t[q] = rt
                    indeg[q] -= 1
                    if indeg[q] == 0:
                        heapq.heappush(heaps[recs[q]["eng"]], (-prio[q], q))
            return assign, max(fin), starts

        _, m0, starts = run(10 ** 6, sorted(job_ops), False)
        best = None
        for acq in (sorted(job_ops), starts):
            try:
                a, m, _ = run(nbanks, acq)
            except AssertionError:
                continue
            print("[plan] ops=%d jobs=%d unconstrained=%.1f planned=%.1f us" % (n, len(job_ops), m0, m))
            pr = [set(p) for p in preds]
            self._chain_banks(recs, pr, a, acq)
            _, mr = self._listsched(recs, pr)
            print("[plan]   -> real list-schedule with this bank plan: %.1f us" % mr)
            if best is None or mr < best[1]:
                best = (a, mr, list(acq))
        return best[0], best[2]

    def _commit(self, r, deps):
        eng, chan = r["eng"], r["chan"]
        idx = len(self.ops)
        op = dict(eng=eng, fn=r["fn"], chan=chan, deps=[], sig=False, idx=idx)
        dom = ("dma", chan) if chan is not None else eng
        op["dom"] = dom
        lst = self.domain_ops.setdefault(dom, [])
        op["pos"] = len(lst) + 1
        lst.append(idx)
        seen = self.seen[eng]
        best = {}
        for d in deps:
            o = self.ops[d]
            if o["dom"] == "pe" and eng == "pe" and chan is None:
                continue
            if seen.get(o["dom"], 0) >= o["pos"]:
                continue
            if best.get(o["dom"], (0, None))[0] < o["pos"]:
                best[o["dom"]] = (o["pos"], d)
        for dom_, (pos, d) in best.items():
            op["deps"].append(d)
            self.ops[d]["sig"] = True
            for k, v in self.ops[d]["vc"].items():
                if seen.get(k, 0) < v:
                    seen[k] = v
        vc = dict(seen)
        vc[dom] = op["pos"]
        if chan is None and eng == "pe":
            seen[dom] = op["pos"]
        op["vc"] = vc
        self.ops.append(op)
        return idx

    def barrier(self):
        self._flush()
        last = [lst[-1] for lst in self.domain_ops.values()]
        for e in ENGS:
            self._commit(dict(eng=e, fn=(lambda eng: None), chan=None), last)

    def emit(self, final_wait_ops=(), sem_stack=None):
        nc = self.nc
        self._flush()
        final_wait_ops = [self.rec2op[r] for r in final_wait_ops]
        if not hasattr(self, "_em"):
            self._em = dict(lo=0, semval={}, cnt={}, sems={}, own=None)
        em = self._em
        if sem_stack is None:
            if em["own"] is None:
                em["own"] = contextlib.ExitStack()
            sem_stack = em["own"]
        lo, hi = em["lo"], len(self.ops)
        semval, cnt, sems = em["semval"], em["cnt"], em["sems"]
        for op in self.ops[lo:hi]:
            if op["chan"] is not None:
                cnt[op["dom"]] = cnt.get(op["dom"], 0) + 16
                semval[op["idx"]] = cnt[op["dom"]]
                op["sig"] = True
            elif op["sig"]:
                cnt[op["dom"]] = cnt.get(op["dom"], 0) + 1
                semval[op["idx"]] = cnt[op["dom"]]
        for d in cnt:
            if d not in sems:
                nm = "s_" + (d if isinstance(d, str) else "dma_" + str(d[1]))
                sems[d] = sem_stack.enter_context(nc.semaphore(nm))
        with nc.Block() as block:
            per_eng = {e: [o for o in self.ops[lo:hi] if o["eng"] == e] for e in ENGS}
            fin = list(final_wait_ops)

            def run(eng_obj, ops, is_last=False):
                for op in ops:
                    for d in op["deps"]:
                        o = self.ops[d]
                        assert d in semval, ("dependency on an op that was emitted without a signal", d)
                        eng_obj.wait_ge(sems[o["dom"]], semval[d])
                    ins = op["fn"](eng_obj)
                    if op["sig"] and ins is not None:
                        ins.then_inc(sems[op["dom"]], 16 if op["chan"] is not None else 1)
                if is_last:
                    for d in fin:
                        o = self.ops[d]
                        eng_obj.wait_ge(sems[o["dom"]], semval[d])

            @block.sync
            def _(e):
                run(e, per_eng["sp"], is_last=True)

            @block.tensor
            def _(e):
                run(e, per_eng["pe"])

            @block.vector
            def _(e):
                run(e, per_eng["dve"])

            @block.scalar
            def _(e):
                run(e, per_eng["act"])

            @block.gpsimd
            def _(e):
                run(e, per_eng["pool"])
        em["lo"] = hi


def _fsz(ap):
    n = 1
    for d in ap.shape[1:]:
        n *= d
    return n


def _wc(fn, cost):
    fn.cost = cost
    return fn


PE_SCALE = 1.0
LS_K = 6
LS_EPS = 1e-9
RELAX = set()


def MM(out, lhsT, rhs, start=True, stop=True):
    c = (0.05 + 0.00045 * _fsz(out)) * (1.6 if lhsT.shape[0] <= 64 else 1.0)
    return _wc(lambda e: e.matmul(out, lhsT=lhsT, rhs=rhs, start=start, stop=stop), max(0.07, c) * PE_SCALE)


def TR(out, in_, ident):
    return _wc(lambda e: e.transpose(out=out, in_=in_, identity=ident), 0.09 * PE_SCALE)


def ACTF(out, in_, func, bias=None, scale=None, accum_out=None):
    kw = {}
    if bias is not None:
        kw["bias"] = bias
    if scale is not None:
        kw["scale"] = scale
    if accum_out is not None:
        kw["accum_out"] = accum_out
    return _wc(lambda e: e.activation(out=out, in_=in_, func=func, **kw), 0.2 + _fsz(out) / 1100.0)


def TT(out, in0, in1, op):
    return _wc(lambda e: e.tensor_tensor(out=out, in0=in0, in1=in1, op=op), 0.12 + _fsz(out) / 950.0)


def TS(out, in0, s1, s2=None, op0=ALU.mult, op1=None):
    c = 0.12 + _fsz(out) / 950.0
    if op1 is None:
        return _wc(lambda e: e.tensor_scalar(out=out, in0=in0, scalar1=s1, scalar2=None, op0=op0), c)
    return _wc(lambda e: e.tensor_scalar(out=out, in0=in0, scalar1=s1, scalar2=s2, op0=op0, op1=op1), c)


def STT(out, in0, scalar, in1, op0, op1):
    return _wc(lambda e: e.scalar_tensor_tensor(out=out, in0=in0, scalar=scalar, in1=in1, op0=op0, op1=op1), 0.12 + _fsz(out) / 500.0)


def CP(out, in_):
    return _wc(lambda e: e.tensor_copy(out=out, in_=in_), 0.12 + _fsz(out) / 950.0)


def RED(out, in_):
    return _wc(lambda e: e.tensor_reduce(out=out, in_=in_, axis=AX.X, op=ALU.add), 0.12 + _fsz(in_) / 950.0)


def DMA(out, in_):
    return _wc(lambda e: e.dma_start(out=out, in_=in_), 2.0 + _fsz(out) * 128 * 4 / 150e3)


def MSET(ap, v):
    return lambda e: e.memset(ap, v)


def bcl(ap, n):
    return bass.AP(ap.tensor, ap.offset, [list(a) for a in ap.ap] + [[0, n]])


def bcm(ap, n):
    a = [list(x) for x in ap.ap]
    return bass.AP(ap.tensor, ap.offset, [a[0], [0, n]] + a[1:])


class Ctx:
    pass


BANK_ASSIGN = {}
NBANK = 8


def common_setup(nc, S, st, C, dr, layer, ncb, pfx="", nbuf=1, nx=2):
    sb = lambda name, shape, dt=F32: st.enter_context(nc.sbuf_tensor("sb_" + pfx + name, shape, dt))
    C.banks = [st.enter_context(nc.psum_tensor(f"{pfx}pb{i}", [128, 512], F32)) for i in range(NBANK)]
    C.bi = 0
    assign = BANK_ASSIGN.get(pfx)

    def bank():
        j = C.bi
        C.bi += 1
        if assign is None:
            return C.banks[j % NBANK], ("pbj", j)
        return C.banks[assign[0][j]], ("pbj", j)
    C.bank = bank
    C.cfm = sb("cfm", [128, NC_FM])
    C.cbc = sb("cbc", [128, ncb])
    C.cT = sb("cT", [128, 8, 2])
    C.condT = sb("condT", [128, 8, 2])
    C.ident = sb("ident", [128, 128])
    C.identb = sb("identb", [128, 128], BF16)
    C.ones = sb("ones", [128, 128])
    C.mhalf = sb("mhalf", [128, 16])
    C.stages = [sb(f"stage{i}", [128, 2, 1024]) for i in range(nbuf)]
    C.stage = C.stages[0]
    C.modrow = sb("modrow", [2, 1024])
    C.modT = sb("modT", [128, 24, 2])
    C.A = sb("Aff", [128, 8, 2])
    C.diag = sb("diag", [128, 2, 128])
    C.gate_bc = sb("gate_bc", [128, 1024])
    C.nx = nx
    C.x = sb("xbuf", [128, nx, 1024])
    C.xns = [sb(f"xn{i}", [128, 1024]) for i in range(nbuf)]
    C.hTs = [sb(f"hT{i}", [128, 8, 128], BF16) for i in range(nbuf)]
    C.ys = [sb(f"ybf{i}", [128, 1024], BF16) for i in range(nbuf)]
    C.yTs = [sb(f"yT{i}", [128, 8, 128], BF16) for i in range(nbuf)]
    C.sms = [sb(f"small{i}", [128, 64]) for i in range(nbuf)]
    C.tmps = [sb(f"tmpx{i}", [128, 512]) for i in range(nbuf)]
    C.xn, C.hT, C.y, C.yT, C.sm, C.tmp = C.xns[0], C.hTs[0], C.ys[0], C.yTs[0], C.sms[0], C.tmps[0]

    S.dma(DMA(C.cfm[:], dr["cfm"]), writes=["cfm"], chan="c_cfm")
    S.dma(DMA(C.cbc[:], dr["cbc"]), writes=["cbc"], chan="c_cbc")
    S.dma(DMA(C.cT[:], dr["cT"]), writes=["cT"], chan="c_cT")
    S.pool(MSET(C.ident[:], 0.0), writes=["ident"])
    S.pool(lambda e: e.affine_select(out=C.ident[:], in_=C.ident[:], pattern=[[-1, 128]], compare_op=ALU.not_equal,
                                     fill=1.0, base=0, channel_multiplier=1), reads=["ident"], writes=["ident"])
    S.dp(CP(C.identb[:], C.ident[:]), reads=["ident"], writes=["identb"])
    S.pool(MSET(C.ones[:], 1.0), writes=["ones"])
    S.pool(MSET(C.mhalf[:], -0.5), writes=["mhalf"])
    S.act(ACTF(C.condT[:], C.cT[:], AF.Tanh, scale=0.5), reads=["cT"], writes=["condT"])
    S.dve(TS(C.condT[:], C.condT[:], 0.5, 0.5, ALU.mult, ALU.add), reads=["condT"], writes=["condT"])
    S.dve(TT(C.condT[:], C.condT[:], C.cT[:], ALU.mult), reads=["condT", "cT"], writes=["condT"])
    adaw = dr["ada_w"]
    cb0 = 8 if layer == 0 else 81
    slots = [(stg, a, h) for stg in C.stages for a in range(2) for h in range(2)]
    n = 0
    for cg in range(6):
        b0, k0 = bank()
        for k in range(8):
            si = n % len(slots)
            stg, a, h = slots[si]
            n += 1
            sv = stg[:, a, h * 512:(h + 1) * 512]
            S.dma(DMA(sv, adaw[k * 128:(k + 1) * 128, cg * 512:(cg + 1) * 512]), writes=[("stage", si)], chan=f"stg{si}")
            S.pe(MM(b0[0:2, :], C.condT[:, k, :], sv, k == 0, k == 7), reads=["condT", ("stage", si)], writes=[k0])
        mr = C.modrow[0:2, (cg % 2) * 512:(cg % 2 + 1) * 512]
        S.act(ACTF(mr, b0[0:2, :], AF.Copy), reads=[k0], writes=[("modrow", cg % 2)])
        bt, kt = bank()
        for q in range(4):
            S.pe(TR(bt[:, 2 * q:2 * q + 2], C.modrow[0:2, (cg % 2) * 512 + q * 128:(cg % 2) * 512 + (q + 1) * 128], C.ident[0:2, 0:2]),
                 reads=[("modrow", cg % 2), "ident"], writes=[kt])
        S.dve(TT(C.modT[:, cg * 4:(cg + 1) * 4, :], bt[:, 0:8].rearrange("p (k s) -> p k s", s=2),
                 bcl(C.cfm[:, cb0 + cg * 4:cb0 + cg * 4 + 4], 2), ALU.add), reads=[kt, "cfm"], writes=["modT"])
    ng0 = 0 if layer == 0 else 73
    S.dve(TS(C.A[:], C.modT[:, 8:16, :], 1.0, None, ALU.add), reads=["modT"], writes=["Aff"])
    S.dve(TT(C.A[:], C.A[:], bcl(C.cfm[:, ng0:ng0 + 8], 2), ALU.mult), reads=["Aff", "cfm"], writes=["Aff"])


def seq_setup(S, C, s):
    bks = [C.bank(), C.bank()]
    for k in range(8):
        sl = k % 2
        S.dve(TS(C.diag[:, sl, :], C.ident[:], C.modT[:, 16 + k, s:s + 1], None, ALU.mult),
              reads=["ident", "modT"], writes=[("diag", sl)])
        b, kb = bks[k // 4]
        S.pe(MM(b[:, (k % 4) * 128:(k % 4 + 1) * 128], C.ones[:], C.diag[:, sl, :]),
             reads=["ones", ("diag", sl)], writes=[kb])
    S.dve(CP(C.gate_bc[:, 0:512], bks[0][0][:]), reads=[bks[0][1]], writes=["gate_bc"])
    S.act(ACTF(C.gate_bc[:, 512:1024], bks[1][0][:], AF.Copy), reads=[bks[1][1]], writes=["gate_bc"])


def norm_and_transpose(S, C, xs, kx, s, par=0):
    xn, hT, sm = C.xns[par], C.hTs[par], C.sms[par]
    ss = sm[:, 0:1]
    S.act(ACTF(xn[:], xs, AF.Square), reads=[kx], writes=[("xn", par)])
    S.dve(RED(ss, xn[:]), reads=[("xn", par)], writes=[("sm0", par)])
    S.dve(TS(sm[:, 1:2], ss, 1.0 / D, 1e-6, ALU.mult, ALU.add), reads=[("sm0", par)], writes=[("sm1", par)])
    S.pool(TT(sm[:, 2:3], sm[:, 1:2], C.mhalf[:, 0:1], ALU.pow), reads=[("sm1", par), "mhalf"], writes=[("sm2", par)])
    S.act(ACTF(xn[:], xs, AF.Identity, scale=sm[:, 2:3]), reads=[kx, ("sm2", par)], writes=[("xn", par)])
    for half in range(2):
        b, kb = C.bank()
        for q in range(4):
            k = half * 4 + q
            S.pe(TR(b[:, q * 128:(q + 1) * 128], xn[:, k * 128:(k + 1) * 128], C.ident[:]),
                 reads=[("xn", par), "ident"], writes=[kb])
        for q in range(4):
            k = half * 4 + q
            if q % 2 == 0:
                S.act(ACTF(hT[:, k, :], b[:, q * 128:(q + 1) * 128], AF.Identity,
                           bias=C.modT[:, k, s:s + 1], scale=C.A[:, k, s:s + 1]),
                      reads=[kb, "modT", "Aff"], writes=[("hT", par, k)])
            else:
                S.dve(TS(hT[:, k, :], b[:, q * 128:(q + 1) * 128], C.A[:, k, s:s + 1], C.modT[:, k, s:s + 1],
                         ALU.mult, ALU.add), reads=[kb, "modT", "Aff"], writes=[("hT", par, k)])


def out_proj_residual(S, C, W, slot, kx, par=0):
    y, yT, tmp = C.ys[par], C.yTs[par], C.tmps[par]
    tb_, tk_ = C.bank()
    tv = tb_[:].bitcast(BF16)
    for k in range(8):
        S.pe(TR(tv[:, k * 128:(k + 1) * 128], y[:, k * 128:(k + 1) * 128], C.identb[:]),
             reads=[("y", par), "identb"], writes=[tk_])
    S.copy(yT[:], tv.rearrange("p (k t) -> p k t", k=8), reads=[tk_], writes=[("yT0", par), ("yT1", par)])
    for cg in range(2):
        b, kb = C.bank()
        for k in range(8):
            S.pe(MM(b[:], yT[:, k, :], W[:, k, cg * 512:(cg + 1) * 512], k == 0, k == 7),
                 reads=[("yT0", par), ("yT1", par), ("W", k)], writes=[kb])
        S.dve(TT(tmp[:], b[:], C.gate_bc[:, cg * 512:(cg + 1) * 512], ALU.mult), reads=[kb, "gate_bc"], writes=[("tmp", par)])
        xv = C.x[:, slot, cg * 512:(cg + 1) * 512]
        S.dp(TT(xv, xv, tmp[:], ALU.add), reads=[("tmp", par), kx], writes=[kx])


def group_ln(S, C, src, ksrc, dst, g_bc, b_bc, eps, pfx, kdst="lnout"):
    sq = C.lnsq
    smt = C.lnsm[pfx]
    s1, s2, m, msq, var, rstd, nmr = (smt[:, 8 * i:8 + 8 * i] for i in range(7))
    src3 = src.rearrange("p (g d) -> p g d", g=8)
    dst3 = dst.rearrange("p (g d) -> p g d", g=8)
    K = lambda n: pfx + n
    S.dve(RED(s1, src3), reads=[ksrc], writes=[K("s1")])
    S.act(ACTF(sq[:], src, AF.Square), reads=[ksrc], writes=[("tmp", 0)])
    S.dve(RED(s2, sq[:].rearrange("p (g d) -> p g d", g=8)), reads=[("tmp", 0)], writes=[K("s2")])
    S.dve(TS(m, s1, 1.0 / 64, None, ALU.mult), reads=[K("s1")], writes=[K("m")])
    S.dve(TT(msq, m, m, ALU.mult), reads=[K("m")], writes=[K("msq")])
    S.dve(STT(var, s2, 1.0 / 64, msq, ALU.mult, ALU.subtract), reads=[K("s2"), K("msq")], writes=[K("var")])
    S.dve(TS(var, var, eps, None, ALU.add), reads=[K("var")], writes=[K("var")])
    S.pool(TT(rstd, var, C.mhalf[:, 0:8], ALU.pow), reads=[K("var"), "mhalf"], writes=[K("rstd")])
    S.dve(STT(nmr, m, -1.0, rstd, ALU.mult, ALU.mult), reads=[K("m"), K("rstd")], writes=[K("nmr")])
    S.dve(TT(dst3, src3, bcl(rstd, 64), ALU.mult), reads=[ksrc, K("rstd")], writes=[kdst])
    S.dp(TT(dst3, dst3, bcl(nmr, 64), ALU.add), reads=[kdst, K("nmr")], writes=[kdst])
    S.dp(TT(dst, dst, g_bc, ALU.mult), reads=[kdst, "cbc"], writes=[kdst])
    S.dp(TT(dst, dst, b_bc, ALU.add), reads=[kdst, "cbc"], writes=[kdst])


def phaseA(nc, S, st, dr, x_out):
    C = Ctx()
    sb = lambda name, shape, dt=F32: st.enter_context(nc.sbuf_tensor("sb_a" + name, shape, dt))
    common_setup(nc, S, st, C, dr, 0, 2048, "a")
    bank = C.bank
    cfm = C.cfm
    W = sb("Win0", [128, 8, 3712], BF16)
    Wo = sb("Wout0", [128, 8, 1024], BF16)
    lora = sb("lora", [128, 512])
    sgw = sb("sgw", [128, 8, 128], BF16)
    mk4 = sb("mk4", [128, 512])
    mkn = sb("mkn", [128, 512])
    ones4 = sb("ones4", [128, 512])
    bones = sb("bones", [128, 128], BF16)
    bo2 = sb("bo2", [128, 2], BF16)
    dc = sb("dconst", [128, 16])
    PA = sb("PA", [128, 13, 129])
    PAprev = sb("PAprev", [128, 13, 1])
    PM = sb("PM", [128, 13, 128])
    TW = sb("TW", [128, 128])
    f4 = lambda name: sb(name, [128, 4, 128])
    th, cs, E1, E2, E3, tha, kx, kp, b2, bt32 = (f4(n) for n in
        ("th", "cs", "E1", "E2", "E3", "tha", "kx", "kp", "b2", "bt32"))
    rn, kt32 = th, cs
    sqb = sb("sqb", [128, 4, 128], BF16)
    AR = sb("AR", [128, 4, 2, 128], BF16)
    btl = sb("btl", [128, 4, 128], BF16)
    ktl = sb("ktl", [128, 4, 128], BF16)
    khb = sb("khb", [128, 4, 128], BF16)
    bhb = sb("bhb", [128, 4, 128], BF16)
    rkb = sb("rkb", [128, 4, 128], BF16)
    kbT = sb("kbT", [128, 1024], BF16)
    khT, bhT = kbT[:, 0:512], kbT[:, 512:1024]
    vT = sb("vT", [128, 512], BF16)
    bonus = sb("bonus", [128, 8])
    MT = sb("MT", [128, 8, 512], BF16)
    Pn = sb("Pn", [128, 2, 8, 128], BF16)
    PT = sb("PTt", [128, 2, 8, 128], BF16)
    TTt = sb("TTt", [128, 8, 128], BF16)
    Rb = sb("Rb", [128, 512], BF16)
    Ub = sb("Ub", [128, 512], BF16)
    S32 = sb("S32", [128, 4, 64])
    Stmp = sb("Stmp", [128, 4, 64])
    SB = sb("SBD", [128, 4, 128], BF16)
    ua = sb("ua", [128, 512])
    vln = sb("vln", [128, 512])
    vlb = sb("vlb", [128, 512], BF16)
    tzt, zst = C.stage, C.stage
    ya = vln
    C.lnsq = C.tmp
    C.lnsm = {"g": sb("lnsm_g", [128, 56]), "a": sb("lnsm_a", [128, 56])}

    S.dma(DMA(lora[:], dr["lora"]), writes=["lora"], chan="c_lora")
    S.dma(DMA(sgw[:], dr["sgwT"]), writes=["sgw"], chan="wq_sg", eng="pool")
    nq = 0
    for k in range(8):
        for c0 in range(0, 3712, 928):
            S.dma(DMA(W[:, k, c0:c0 + 928], dr["e_w_in"][k * 128:(k + 1) * 128, c0:c0 + 928]),
                  writes=[("W0", k, c0), ("wqchain", nq % 4)], chan=f"wq{nq % 4}", eng="pool")
            nq += 1
    for k in range(8):
        S.dma(DMA(Wo[:, k, :], dr["e_w_out"][k * 128:(k + 1) * 128, :]), writes=[("W", k), ("wqchain", nq % 4)], chan=f"wq{nq % 4}", eng="pool")
        nq += 1
    S.pool(MSET(mk4[:], 1.0), writes=["mk4"])
    for q in range(4):
        S.pool(lambda e, q=q: e.affine_select(out=mk4[:, q * 128:(q + 1) * 128], in_=mk4[:, q * 128:(q + 1) * 128],
                                              pattern=[[1, 128]], compare_op=(ALU.is_gt if q % 2 == 0 else ALU.is_ge),
                                              fill=0.0, base=0, channel_multiplier=-1), reads=["mk4"], writes=["mk4"])
    S.pool(MSET(mkn[:], 1.0), writes=["mkn"])
    for q in range(4):
        S.pool(lambda e, q=q: e.affine_select(out=mkn[:, q * 128:(q + 1) * 128], in_=mkn[:, q * 128:(q + 1) * 128],
                                              pattern=[[-1, 128]], compare_op=ALU.is_gt, fill=0.0, base=0,
                                              channel_multiplier=1), reads=["mkn"], writes=["mkn"])
    S.pool(MSET(ones4[:], 1.0), writes=["ones4"])
    S.pool(MSET(ones4[:].rearrange("p (j t) -> p j t", j=4)[:, :, 0:1], 0.0), reads=["ones4"], writes=["ones4"])
    S.pool(MSET(bones[:], 0.0), writes=["bones"])
    S.pool(MSET(bones[0:64, 0:64], 1.0), reads=["bones"], writes=["bones"])
    S.pool(MSET(bones[64:128, 64:128], 1.0), reads=["bones"], writes=["bones"])
    S.pool(MSET(bo2[:], 0.0), writes=["bo2"])
    S.pool(MSET(bo2[0:64, 0:1], 1.0), reads=["bo2"], writes=["bo2"])
    S.pool(MSET(bo2[64:128, 1:2], 1.0), reads=["bo2"], writes=["bo2"])
    for g in range(8):
        S.pool(lambda e, g=g: e.affine_select(out=sgw[:, g, :], in_=sgw[:, g, :], pattern=[[1, 128]], compare_op=ALU.is_ge,
                                              fill=0.0, base=0, channel_multiplier=-1), reads=["sgw"], writes=["sgw"])
    S.dve(TS(dc[:, 0:4], cfm[:, 45:49], 0.5, None, ALU.mult), reads=["cfm"], writes=["dc"])
    S.dve(TS(dc[:, 4:8], cfm[:, 49:53], 0.5, None, ALU.mult), reads=["cfm"], writes=["dc"])
    S.dve(TS(dc[:, 8:12], cfm[:, 57:61], 0.5, None, ALU.mult), reads=["cfm"], writes=["dc"])
    S.dve(TS(dc[:, 12:16], cfm[:, 57:61], -0.5, 1.0, ALU.mult, ALU.add), reads=["cfm"], writes=["dc"])

    lnxg, lnxb = C.cbc[:, 0:512], C.cbc[:, 512:1024]
    sgg, sgb = C.cbc[:, 1024:1536], C.cbc[:, 1536:2048]
    xin = dr["x"]
    steps = [(s, c) for s in range(2) for c in range(NCH)][:DBG_STEPS]
    out_ops = []

    def load_x(i):
        s, c = steps[i]
        S.dma(DMA(C.x[:, i % 2, :], xin[s, c * 128:(c + 1) * 128, :]), writes=[("x", i % 2)], chan=f"xin{i % 2}")

    load_x(0)
    for i, (s, c) in enumerate(steps):
      slot = i % 2
      xs = C.x[:, slot, :]
      kxs = ("x", slot)
      try:
        if i == 0:
            dump(S, "mkn", mkn[:], "mkn", 512)
            dump(S, "mk4", mk4[:], "mk4", 512)
            dump(S, "ones4", ones4[:], "ones4", 512)
        cp(1)
        if i + 1 < len(steps):
            load_x(i + 1)
        if c == 0:
            seq_setup(S, C, s)
            S.pool(MSET(S32[:], 0.0), writes=["S32"])
            S.pool(MSET(SB[:], 0.0), writes=["SB"])
            S.pool(MSET(PAprev[:], 0.0), writes=["PAprev"])
        cp(11)
        norm_and_transpose(S, C, xs, kxs, s)
        hk = [("hT", 0, k) for k in range(8)]
        cp(2)
        if i == 0:
            dump(S, "hT", C.hT[:, 0:2, :].rearrange("p k t -> p (k t)"), hk, 256)
        S.dp(CP(PA[:, :, 0:1], PAprev[:]), reads=["PAprev"], writes=["PAc0"])
        for g0 in range(0, 13, 4):
            nb = min(4, 13 - g0)
            b, kb = bank()
            for q in range(nb):
                blk = g0 + q
                for k in range(8):
                    S.pe(MM(b[:, q * 128:(q + 1) * 128], W[:, k, blk * 128:(blk + 1) * 128], C.hT[:, k, :], k == 0, k == 7),
                         reads=[("W0", k, 0), ("W0", k, 928), ("W0", k, 1856), ("W0", k, 2784), ("hT", 0, k)], writes=[kb])
            src = b[:, 0:nb * 128].rearrange("p (q t) -> p q t", q=nb)
            S.copy(PA[:, g0:g0 + nb, 1:129], src, reads=[kb], writes=[("PA", g0)])
        pak = [("PA", g0) for g0 in range(0, 13, 4)] + ["PAc0"]
        cp(3)
        ub, ukb = bank()
        for k in range(8):
            S.pe(MM(ub[:], C.hT[:, k, :], W[:, k, 1664:2176], k == 0, k == 7), reads=[("W0", k, 0), ("W0", k, 928), ("W0", k, 1856), ("W0", k, 2784), ("hT", 0, k)], writes=[ukb])
        S.copy(ua[:], ub[:], reads=[ukb], writes=["ua"])
        vb, vkb = bank()
        for k in range(8):
            S.pe(MM(vb[:], C.hT[:, k, :], W[:, k, 2176:2688], k == 0, k == 7), reads=[("W0", k, 0), ("W0", k, 928), ("W0", k, 1856), ("W0", k, 2784), ("hT", 0, k)], writes=[vkb])
        group_ln(S, C, vb[:], vkb, vln[:], sgg, sgb, 1e-5, "g")
        S.copy(vlb[:], vln[:], reads=["lnout"], writes=["vlb"])
        for zc in range(2):
            zb, zkb = bank()
            for k in range(8):
                S.pe(MM(zb[:], C.hT[:, k, :], W[:, k, 2688 + zc * 512:3200 + zc * 512], k == 0, k == 7),
                     reads=[("W0", k, 0), ("W0", k, 928), ("W0", k, 1856), ("W0", k, 2784), ("hT", 0, k)], writes=[zkb])
            S.act(ACTF(tzt[:, 0, zc * 512:(zc + 1) * 512], zb[:], AF.Tanh, scale=0.5), reads=[zkb], writes=[("tz", zc)])
            S.dve(STT(zst[:, 1, zc * 512:(zc + 1) * 512], tzt[:, 0, zc * 512:(zc + 1) * 512], 1.0, zb[:], ALU.add, ALU.mult),
                  reads=[zkb, ("tz", zc)], writes=[("zs", zc)])
        mb, mkb = bank()
        for g in range(8):
            S.pe(MM(mb[:, g * 64:(g + 1) * 64], sgw[:, g, :], vlb[:, g * 64:(g + 1) * 64]), reads=["sgw", "vlb"], writes=[mkb])
        S.dve(TT(vln[:].rearrange("p (g d) -> p g d", g=8), mb[:].rearrange("p (g d) -> p g d", g=8),
                 bcl(cfm[:, 65:73], 64), ALU.add), reads=[mkb, "cfm", "vlb"], writes=["lnout"])
        S.dp(TT(vln[:], vln[:], ua[:], ALU.mult), reads=["lnout", "ua"], writes=["lnout"])
        S.dve(STT(C.y[:, 512:1024], vln[:], 0.5, zst[:, 1, 512:1024], ALU.mult, ALU.mult), reads=["lnout", ("zs", 1)], writes=[("y", 0)])
        cp(4)
        S.dp(TT(PM[:], PA[:, :, 0:128], PA[:, :, 1:129], ALU.subtract), reads=pak, writes=["PM"])
        S.dve(TT(PM[:], PM[:], bcl(cfm[:, 32:45], 128), ALU.mult), reads=["PM", "cfm"], writes=["PM"])
        S.dp(TT(PM[:], PM[:], PA[:, :, 1:129], ALU.add), reads=["PM"] + pak, writes=["PM"])
        S.dp(CP(PAprev[:], PA[:, :, 128:129]), reads=pak, writes=["PAprev"])
        r_, k_, v_ = PM[:, 0:4, :], PM[:, 4:8, :], PM[:, 8:12, :]
        S.act(ACTF(TW[0:64, :], PM[0:64, 12, :], AF.Tanh), reads=["PM"], writes=["TW"])
        S.act(ACTF(TW[64:128, :], PM[64:128, 12, :], AF.Copy), reads=["PM"], writes=["TW"])
        wb, wkb = bank()
        ab, akb = bank()
        for j in range(4):
            S.pe(MM(wb[:, j * 128:(j + 1) * 128], lora[0:64, j * 128:(j + 1) * 128], TW[0:64, :]), reads=["lora", "TW"], writes=[wkb])
        for j in range(4):
            S.pe(MM(ab[:, j * 128:(j + 1) * 128], lora[64:128, j * 128:(j + 1) * 128], TW[64:128, :]), reads=["lora", "TW"], writes=[akb])
        for j in range(4):
            S.act(ACTF(th[:, j, :], wb[:, j * 128:(j + 1) * 128], AF.Tanh, bias=dc[:, j:j + 1], scale=0.5),
                  reads=[wkb, "dc"], writes=["th"])
        for j in range(4):
            S.act(ACTF(tha[:, j, :], ab[:, j * 128:(j + 1) * 128], AF.Tanh, bias=dc[:, 4 + j:5 + j], scale=0.5),
                  reads=[akb, "dc"], writes=["tha"])
        thf = th[:].rearrange("p j t -> p (j t)")
        csf = cs[:].rearrange("p j t -> p (j t)")
        S.dve(TS(thf, thf, C1, C1, ALU.mult, ALU.add), reads=["th"], writes=["th"])
        S.dve(lambda e: e.tensor_tensor_scan(out=csf, data0=ones4[:], data1=thf, initial=0.0, op0=ALU.mult, op1=ALU.add),
              reads=["th", "ones4"], writes=["cs"])
        S.act(ACTF(E1[:], cs[:], AF.Exp), reads=["cs"], writes=["E1"])
        S.act(ACTF(E2[:], cs[:], AF.Exp, scale=-1.0), reads=["cs"], writes=["E2"])
        S.dp(TT(th[:], cs[:], th[:], ALU.subtract), reads=["cs", "th"], writes=["th"])
        S.act(ACTF(E3[:], th[:], AF.Exp), reads=["th"], writes=["E3"])
        S.dp(TT(kx[:], k_, bcl(cfm[:, 53:57], 128), ALU.mult), reads=["PM", "cfm"], writes=["kx"])
        S.act(ACTF(sqb[:], kx[:], AF.Square), reads=["kx"], writes=["sqb"])
        sb_, skb = bank()
        for j in range(4):
            S.pe(MM(sb_[:, j * 128:(j + 1) * 128], bones[:], sqb[:, j, :]), reads=["bones", "sqb"], writes=[skb])
        rnf = rn[:].rearrange("p j t -> p (j t)")
        S.dve(TS(rnf, sb_[:], 1e-24, None, ALU.max), reads=[skb], writes=["th"])
        S.act(ACTF(rnf, rnf, AF.Ln), reads=["th"], writes=["th"])
        S.act(ACTF(rnf, rnf, AF.Exp, scale=-0.5), reads=["th"], writes=["th"])
        S.dp(TT(kx[:], kx[:], rn[:], ALU.mult), reads=["kx", "th"], writes=["kx"])
        S.dve(TT(kp[:], tha[:], bcl(dc[:, 8:12], 128), ALU.mult), reads=["tha", "dc"], writes=["kp"])
        S.dp(TT(kp[:], kp[:], bcl(dc[:, 12:16], 128), ALU.add), reads=["kp", "dc"], writes=["kp"])
        S.dp(TT(kp[:], kp[:], k_, ALU.mult), reads=["kp", "PM"], writes=["kp"])
        S.dve(STT(b2[:], tha[:], 1.0, kx[:], ALU.add, ALU.mult), reads=["tha", "kx"], writes=["b2"])
        S.dve(STT(AR[:, :, 0, :], kx[:], -1.0, E3[:], ALU.mult, ALU.mult), reads=["kx", "E3"], writes=["ARa"])
        S.dve(TT(AR[:, :, 1, :], r_, E1[:], ALU.mult), reads=["PM", "E1"], writes=["ARr"])
        S.dve(STT(bt32[:], b2[:], 0.5, E2[:], ALU.mult, ALU.mult), reads=["b2", "E2"], writes=["bt32"])
        S.dp(TT(kt32[:], kp[:], E2[:], ALU.mult), reads=["kp", "E2"], writes=["cs"])
        S.copy(btl[:], bt32[:], reads=["bt32"], writes=["btl"])
        S.copy(ktl[:], kt32[:], reads=["cs"], writes=["ktl"])
        gL = E1[:, :, 127:128]
        S.dp(TT(khb[:], kt32[:], bcl(E1[:, :, 127], 128), ALU.mult), reads=["cs", "E1"], writes=["khb"])
        S.dp(TT(bhb[:], bt32[:], bcl(E1[:, :, 127], 128), ALU.mult), reads=["bt32", "E1"], writes=["bhb"])
        S.dve(TT(b2[:], r_, kp[:], ALU.mult), reads=["PM", "kp", "b2"], writes=["b2"])
        S.dp(TT(rkb[:], b2[:], bcl(cfm[:, 61:65], 128), ALU.mult), reads=["b2", "cfm"], writes=["rkb"])
        if i == 0:
            dump(S, "PM", PM[:].rearrange("p b t -> p (b t)"), "PM", 1664)
            dump(S, "E1", E1[:].rearrange("p b t -> p (b t)"), "E1", 512)
            dump(S, "kk", kx[:].rearrange("p b t -> p (b t)"), "kx", 512)
            dump(S, "kp", kp[:].rearrange("p b t -> p (b t)"), "kp", 512)
            dump(S, "tha", tha[:].rearrange("p b t -> p (b t)"), "tha", 512)
            dump(S, "ua", ua[:], "ua", 512)
            dump(S, "yb", C.y[:, 512:1024], ("y", 0), 512)
        cp(5)
        tbk, tkk = bank()
        tvk = tbk[:].bitcast(BF16)
        for j in range(4):
            S.pe(TR(tvk[:, j * 128:(j + 1) * 128], khb[:, j, :], C.identb[:]), reads=["khb", "identb"], writes=[tkk])
        for j in range(4):
            S.pe(TR(tvk[:, 512 + j * 128:512 + (j + 1) * 128], bhb[:, j, :], C.identb[:]), reads=["bhb", "identb"], writes=[tkk])
        S.copy(kbT[:], tvk, reads=[tkk], writes=["khT", "bhT"])
        vtb, vtk = bank()
        for j in range(4):
            S.pe(TR(vtb[:, j * 128:(j + 1) * 128], PM[:, 8 + j, :], C.ident[:]), reads=["PM", "ident"], writes=[vtk])
        S.copy(vT[:], vtb[:], reads=[vtk], writes=["vT"])
        bob, bok = bank()
        for j in range(4):
            S.pe(MM(bob[:, 2 * j:2 * j + 2], rkb[:, j, :], bo2[:]), reads=["rkb", "bo2"], writes=[bok])
        S.dve(CP(bonus[:], bob[:, 0:8]), reads=[bok], writes=["bonus"])
        cp(6)
        nbs = [bank(), bank()]
        for h in range(8):
            j, r0 = h // 2, (h % 2) * 64
            nb_, nkb = nbs[h % 2]
            S.pe(MM(nb_[:, j * 128:(j + 1) * 128], AR[r0:r0 + 64, j, 0, :], btl[r0:r0 + 64, j, :]),
                 reads=["ARa", "btl"], writes=[nkb])
        for par in range(2):
            nb_, nkb = nbs[par]
            S.dve(TT(Pn[:, 0, :, :].rearrange("p (a two) t -> p a two t", two=2)[:, :, par, :],
                     nb_[:].rearrange("p (h t) -> p h t", h=4),
                     mkn[:].rearrange("p (h t) -> p h t", h=4), ALU.mult), reads=[nkb, "mkn"], writes=[("P", 0, 0), ("P", 0, 1)])
        for h in range(8):
            j, r0 = h // 2, (h % 2) * 64
            b, kb = bank()
            S.pe(MM(b[:, 0:256], btl[r0:r0 + 64, j, :], AR[r0:r0 + 64, j, :, :].rearrange("p a t -> p (a t)")),
                 reads=["btl", "ARa", "ARr"], writes=[kb])
            S.pe(MM(b[:, 256:512], ktl[r0:r0 + 64, j, :], AR[r0:r0 + 64, j, :, :].rearrange("p a t -> p (a t)")),
                 reads=["ktl", "ARa", "ARr"], writes=[kb])
            S.dve(TT(MT[:, h, :], b[:], mk4[:], ALU.mult), reads=[kb, "mk4"], writes=[("MT", h)])
        if i == 0:
            dump(S, "P0h0", Pn[:, 0, 0, :], [("P", 0, 0)], 128)
            dump(S, "P0h1", Pn[:, 0, 1, :], [("P", 0, 0)], 128)
            dump(S, "P0h5", Pn[:, 0, 5, :], [("P", 0, 1)], 128)
        cp(7)
        for g in range(2):
            hs = slice(g * 4, (g + 1) * 4)
            S.dp(CP(PT[:, 0, hs, :], MT[:, hs, 0:128]), reads=[("MT", h) for h in range(g * 4, g * 4 + 4)], writes=[("PT", 0, g)])
            S.dp(TT(TTt[:, hs, :], MT[:, hs, 0:128], bcm(C.identb[:], 4), ALU.add),
                   reads=[("MT", h) for h in range(g * 4, g * 4 + 4)] + ["identb"], writes=[("TT", g)])
        for lv in range(1, 7):
            cur, prv = lv % 2, (lv - 1) % 2
            for g in range(2):
                pb_, pkb = bank()
                for q in range(4):
                    h = g * 4 + q
                    S.pe(MM(pb_[:, q * 128:(q + 1) * 128], PT[:, prv, h, :], Pn[:, prv, h, :]),
                         reads=[("PT", prv, g), ("P", prv, g)], writes=[pkb])
                S.copy(Pn[:, cur, g * 4:(g + 1) * 4, :], pb_[:].rearrange("p (h t) -> p h t", h=4),
                       reads=[pkb], writes=[("P", cur, g)])
                if lv < 6:
                    tb_, tkb = bank()
                    for q in range(4):
                        h = g * 4 + q
                        S.pe(MM(tb_[:, q * 128:(q + 1) * 128], Pn[:, prv, h, :], PT[:, prv, h, :]),
                             reads=[("PT", prv, g), ("P", prv, g)], writes=[tkb])
                    S.copy(PT[:, cur, g * 4:(g + 1) * 4, :], tb_[:].rearrange("p (h t) -> p h t", h=4),
                           reads=[tkb], writes=[("PT", cur, g)])
                db_, dkb = bank()
                for q in range(4):
                    h = g * 4 + q
                    S.pe(MM(db_[:, q * 128:(q + 1) * 128], Pn[:, cur, h, :], TTt[:, h, :]),
                         reads=[("P", cur, g), ("TT", g)], writes=[dkb])
                S.dve(TT(TTt[:, g * 4:(g + 1) * 4, :], db_[:].rearrange("p (h t) -> p h t", h=4), TTt[:, g * 4:(g + 1) * 4, :], ALU.add),
                      reads=[dkb, ("TT", g)], writes=[("TT", g)])
        cp(8)
        mtk = [("MT", h) for h in range(8)]
        rb_, rkb_ = bank()
        for j in range(4):
            S.pe(MM(rb_[:, j * 128:(j + 1) * 128], AR[:, j, 0, :], SB[:, j, :], True, False),
                 reads=["ARa", "SB"], writes=[rkb_])
            for h in (2 * j, 2 * j + 1):
                S.pe(MM(rb_[:, h * 64:(h + 1) * 64], MT[:, h, 256:384], vT[:, h * 64:(h + 1) * 64], False, h % 2 == 1),
                     reads=[("MT", h), "vT"], writes=[rkb_])
        S.copy(Rb[:], rb_[:], reads=[rkb_], writes=["Rb"])
        ub_, ukb_ = bank()
        for h in range(8):
            S.pe(MM(ub_[:, h * 64:(h + 1) * 64], TTt[:, h, :], Rb[:, h * 64:(h + 1) * 64]), reads=[("TT", h // 4), "Rb"], writes=[ukb_])
        S.copy(Ub[:], ub_[:], reads=[ukb_], writes=["Ub"])
        yb_, ykb_ = bank()
        for j in range(4):
            S.pe(MM(yb_[:, j * 128:(j + 1) * 128], AR[:, j, 1, :], SB[:, j, :], True, False),
                 reads=["ARr", "SB"], writes=[ykb_])
            for h in (2 * j, 2 * j + 1):
                S.pe(MM(yb_[:, h * 64:(h + 1) * 64], MT[:, h, 128:256], Ub[:, h * 64:(h + 1) * 64], False, False),
                     reads=[("MT", h), "Ub"], writes=[ykb_])
                S.pe(MM(yb_[:, h * 64:(h + 1) * 64], MT[:, h, 384:512], vT[:, h * 64:(h + 1) * 64], False, h % 2 == 1),
                     reads=[("MT", h), "vT"], writes=[ykb_])
        snb, snk = bank()
        for j in range(4):
            S.pe(MM(snb[:, j * 128:(j + 1) * 128], kbT[:, 512 + j * 128:512 + (j + 1) * 128], Ub[:, j * 128:(j + 1) * 128], True, False),
                 reads=["bhT", "Ub"], writes=[snk])
            S.pe(MM(snb[:, j * 128:(j + 1) * 128], kbT[:, j * 128:(j + 1) * 128], vT[:, j * 128:(j + 1) * 128], False, True),
                 reads=["khT", "vT"], writes=[snk])
        S.dve(TT(Stmp[:], S32[:], bcl(E1[:, :, 127], 64), ALU.mult), reads=["S32", "E1"], writes=["Stmp"])
        sn3 = snb[:].rearrange("p (j c) -> p j c", j=4)
        S.dve(TT(S32[0:64, :, :], Stmp[0:64, :, :], sn3[0:64, :, 0:64], ALU.add), reads=["Stmp", snk, "S32"], writes=["S32"])
        S.dve(TT(S32[64:128, :, :], Stmp[64:128, :, :], sn3[64:128, :, 64:128], ALU.add), reads=["Stmp", snk, "S32"], writes=["S32"])
        S.copy(SB[0:64, :, 0:64], S32[0:64, :, :], reads=["S32"], writes=["SB"])
        S.copy(SB[64:128, :, 64:128], S32[64:128, :, :], reads=["S32"], writes=["SB"])
        cp(9)
        group_ln(S, C, yb_[:], ykb_, ya[:], lnxg, lnxb, 64e-5, "a")
        S.dve(TT(C.tmp[:].rearrange("p (h d) -> p h d", h=8), vT[:].rearrange("p (h d) -> p h d", h=8), bcl(bonus[:], 64), ALU.mult),
              reads=["vT", "bonus"], writes=[("tmp", 0)])
        S.dp(TT(ya[:], ya[:], C.tmp[:], ALU.add), reads=["lnout", ("tmp", 0)], writes=["lnout"])
        S.dve(STT(C.y[:, 0:512], ya[:], 0.5, zst[:, 1, 0:512], ALU.mult, ALU.mult), reads=["lnout", ("zs", 0)], writes=[("y", 0)])
        if i == 0:
            dump(S, "vT", vT[:], "vT", 512)
            dump(S, "bonus", bonus[:], "bonus", 8)
            dump(S, "Ub", Ub[:], "Ub", 512)
            dump(S, "TT0", TTt[:, 0, :], ("TT", 0), 128)
            dump(S, "MT0", MT[:, 0, :], ("MT", 0), 512)
            dump(S, "y", C.y[:], ("y", 0), 1024)
            dump(S, "S32", S32[:].rearrange("p j v -> p (j v)"), "S32", 256)
        cp(10)
        out_proj_residual(S, C, Wo, slot, kxs)
      except Cut:
        pass
      if True:
        out_ops.append(S.dma(DMA(x_out[s, c * 128:(c + 1) * 128, :], xs), reads=[kxs], writes=[("x1", s, c)], chan=f"xo{slot}"))
    return out_ops


def build_A():
    nc = bass.Bass("TRN2", target_bir_lowering=False)
    dr = {}
    dr["x"] = nc.dram_tensor("x", [2, SEQ, D], F32, kind="ExternalInput").ap()
    dr["cT"] = nc.dram_tensor("cT", [128, 8, 2], F32, kind="ExternalInput").ap()
    dr["ada_w"] = nc.dram_tensor("ada_w0", [D, 3 * D], F32, kind="ExternalInput").ap()
    dr["e_w_in"] = nc.dram_tensor("e_w_in", [D, 3712], F32, kind="ExternalInput").ap()
    dr["e_w_out"] = nc.dram_tensor("e_w_out", [D, D], F32, kind="ExternalInput").ap()
    dr["cfm"] = nc.dram_tensor("cfm", [128, NC_FM], F32, kind="ExternalInput").ap()
    dr["cbc"] = nc.dram_tensor("cbcA", [128, 2048], F32, kind="ExternalInput").ap()
    dr["lora"] = nc.dram_tensor("lora", [128, 512], F32, kind="ExternalInput").ap()
    dr["sgwT"] = nc.dram_tensor("sgwT", [128, 8, 128], F32, kind="ExternalInput").ap()
    x1 = nc.dram_tensor("x1", [2, SEQ, D], F32, kind="ExternalOutput").ap()
    DUMP["ops"] = []
    if DUMP["on"]:
        DUMP["ap"] = nc.dram_tensor("dbg", [128, 16384], F32, kind="ExternalOutput").ap()
        DUMP["col"] = 0
    S = Sched(nc)
    with contextlib.ExitStack() as st:
        outs = phaseA(nc, S, st, dr, x1)
        S.emit(final_wait_ops=outs[-2:] + DUMP["ops"])
    return nc


def fm(v, nblk):
    return np.ascontiguousarray(np.asarray(v, np.float32).reshape(nblk, 128).T)


def host_consts(inp):
    cfm = np.zeros((128, NC_FM), np.float32)
    cfm[:, 0:8] = fm(inp["norm_g"][0], 8)
    cfm[:, 8:32] = fm(inp["ada_b"][0], 24)
    cfm[:, 32:45] = fm(inp["e_mu"][0], 13)
    cfm[:, 45:49] = fm(inp["e_w0"][0], 4)
    cfm[:, 49:53] = fm(inp["e_a0"][0], 4)
    cfm[:, 53:57] = fm(inp["e_k_k"][0], 4)
    cfm[:, 57:61] = fm(inp["e_k_a"][0], 4)
    cfm[:, 61:65] = fm(inp["e_r_k"][0].reshape(-1), 4)
    cfm[:, 65:73] = np.asarray(inp["e_sg_b"][0], np.float32).T
    cfm[:, 73:81] = fm(inp["norm_g"][1], 8)
    cfm[:, 81:105] = fm(inp["ada_b"][1], 24)
    row = np.concatenate([inp["e_lnx_g"][0], inp["e_lnx_b"][0], inp["e_sg_ln_g"][0], inp["e_sg_ln_b"][0],
                          inp["final_g"], inp["o_sinks"][0]]).astype(np.float32)
    cbc = np.ascontiguousarray(np.broadcast_to(row[None, :], (128, row.shape[0])))
    lora = np.ascontiguousarray(np.concatenate([inp["e_w_up"][0], inp["e_a_up"][0]], axis=0).astype(np.float32))
    sgwT = np.ascontiguousarray(np.transpose(np.asarray(inp["e_sg_w"][0], np.float32), (2, 0, 1)))
    return cfm, cbc, lora, sgwT


_CACHE = {}


def run_A(inp):
    if "A" not in _CACHE:
        _CACHE["A"] = build_A()
    nc = _CACHE["A"]
    cfm, cbc, lora, sgwT = host_consts(inp)
    x = np.asarray(inp["x"], np.float32)
    c = np.asarray(inp["c"], np.float32)
    maps = []
    for i in range(NCORES):
        cs = c[2 * i:2 * i + 2]
        cT = np.ascontiguousarray(cs.reshape(2, 8, 128).transpose(2, 1, 0))
        maps.append({"x": np.ascontiguousarray(x[2 * i:2 * i + 2]), "cT": cT,
                     "ada_w0": np.ascontiguousarray(inp["ada_w"][0]), "e_w_in": np.ascontiguousarray(inp["e_w_in"][0]),
                     "e_w_out": np.ascontiguousarray(inp["e_w_out"][0]), "cfm": cfm,
                     "cbcA": np.ascontiguousarray(cbc[:, 0:2048]), "lora": lora, "sgwT": sgwT})
    res = run_bass_kernel_spmd(nc, maps, core_ids=list(range(NCORES)))
    if DUMP["on"]:
        DUMP["data"] = res.results[0]["dbg"]
    return np.concatenate([r["x1"] for r in res.results], axis=0)


SLOPES = [2.0 ** (-8.0 * (h + 1) / 16.0) for h in range(16)]
W1C = 2432


def phaseB(nc, S, st, dr, x_out):
    C = Ctx()
    sb = lambda name, shape, dt=F32: st.enter_context(nc.sbuf_tensor("sb_b" + name, shape, dt))
    NB = 2
    common_setup(nc, S, st, C, dr, 1, 1040, "b", nbuf=NB, nx=3)
    bank = C.bank
    W = sb("Win1", [128, 8, W1C], BF16)
    Wo = sb("Wout1", [128, 8, 1024], BF16)
    E = sb("Ealibi", [128, 2, 16, 128], BF16)
    PTs = [sb(f"PTatt{i}", [128, 2, 16, 128], BF16) for i in range(NB)]
    ex = sb("expt", [128, 4, 512])
    qTs = [sb(f"qT{i}", [128, 8, 2, 128], BF16) for i in range(NB)]
    kT2 = sb("kT2", [128, 2, 3, 128], BF16)
    V1 = sb("V1", [128, 3, 2, 65], BF16)
    esink = sb("esink", [128, 16])
    Dm = sb("Dm", [128, 128])
    qrow = sb("qrow", [128, 128])
    kcol = sb("kcol", [128, 1])
    dens = [sb(f"den{i}", [128, 16]) for i in range(NB)]
    atts = [sb(f"att{i}", [128, 1024]) for i in range(NB)]
    obs = [sb(f"ob{i}", [128, 1024]) for i in range(NB)]
    fg = C.cbc[:, 0:1024]
    C.negs = sb("negs", [128, 16])
    for h in range(16):
        S.pool(MSET(C.negs[:, h:h + 1], -128.0 * SLOPES[h]), writes=["negs"])

    nq = 0
    for k in range(8):
        for c0 in range(0, W1C, 608):
            S.dma(DMA(W[:, k, c0:c0 + 608], dr["o_w_in"][k * 128:(k + 1) * 128, c0:c0 + 608]),
                  writes=[("W1", k, c0), ("wqchain", nq % 4)], chan=f"wq{nq % 4}", eng="pool")
            nq += 1
    for k in range(8):
        S.dma(DMA(Wo[:, k, :], dr["o_w_out"][k * 128:(k + 1) * 128, :]), writes=[("W", k), ("wqchain", nq % 4)], chan=f"wq{nq % 4}", eng="pool")
        nq += 1
    wk = lambda k: [("W1", k, c0) for c0 in range(0, W1C, 608)]
    S.dve(lambda e: e.tensor_tensor_scan(out=qrow[:], data0=C.ones[:], data1=C.ones[:], initial=-1.0, op0=ALU.mult, op1=ALU.add),
          reads=["ones"], writes=["qrow"])
    S.dve(TT(Dm[:], qrow[:], C.ident[:], ALU.mult), reads=["qrow", "ident"], writes=["Dm"])
    S.dve(RED(kcol[:], Dm[:]), reads=["Dm"], writes=["kcol"])
    S.dve(TS(Dm[:], qrow[:], kcol[:, 0:1], None, ALU.subtract), reads=["qrow", "kcol"], writes=["Dm"])
    for h in range(16):
        sl = h % 4
        S.act(ACTF(ex[:, sl, 0:128], Dm[:], AF.Exp, scale=-SLOPES[h]), reads=["Dm"], writes=[("ex", sl)])
        S.pool(lambda e, h=h, sl=sl: e.affine_select(out=E[:, 1, h, :], in_=ex[:, sl, 0:128], pattern=[[1, 128]], compare_op=ALU.is_ge,
                                                     fill=0.0, base=0, channel_multiplier=-1), reads=[("ex", sl)], writes=[("E", h, 1)])
        S.act(ACTF(ex[:, sl, 128:256], Dm[:], AF.Exp, scale=-SLOPES[h], bias=C.negs[:, h:h + 1]), reads=["Dm", "negs"], writes=[("ex2", sl)])
        S.pool(lambda e, h=h, sl=sl: e.affine_select(out=E[:, 0, h, :], in_=ex[:, sl, 128:256], pattern=[[-1, 128]], compare_op=ALU.is_gt,
                                                     fill=0.0, base=0, channel_multiplier=1), reads=[("ex2", sl)], writes=[("E", h, 0)])
    Ek = [("E", h, pc) for h in range(16) for pc in range(2)]
    S.act(ACTF(esink[:], C.cbc[:, 1024:1040], AF.Exp), reads=["cbc"], writes=["esink"])
    S.pool(MSET(V1[:], 1.0), writes=[("V1", r) for r in range(3)])
    for i_ in range(NB):
        S.pool(MSET(qTs[i_][:], 0.0), writes=[("qT", i_, 0), ("qT", i_, 4)])

    xin = dr["x1"]
    steps = [(s, c) for s in range(2) for c in range(NCH)][:DBG_STEPS]
    out_ops = []
    nx = C.nx

    def load_x(i):
        s, c = steps[i]
        S.dma(DMA(C.x[:, i % nx, :], xin[s, c * 128:(c + 1) * 128, :]), reads=[("x1", s, c)], writes=[("x", i % nx)], chan=f"xin{i % nx}")

    load_x(0)
    n_ex = 0
    for i, (s, c) in enumerate(steps):
        slot = i % nx
        par = i % NB
        xs = C.x[:, slot, :]
        kxs = ("x", slot)
        hT, qT, PT, den, att, ob = C.hTs[par], qTs[par], PTs[par], dens[par], atts[par], obs[par]
        tzt = zst = C.stages[par]
        y = C.ys[par]
        if i + 1 < len(steps):
            load_x(i + 1)
        if c == 0:
            seq_setup(S, C, s)
        norm_and_transpose(S, C, xs, kxs, s, par)
        cur, prv = c % 3, (c - 1) % 3
        for g0 in (0, 4, 8):
            nb = 4 if g0 < 8 else 2
            b, kb = bank()
            for q in range(nb):
                blk = g0 + q
                for k in range(8):
                    S.pe(MM(b[:, q * 128:(q + 1) * 128], W[:, k, blk * 128:(blk + 1) * 128], hT[:, k, :], k == 0, k == 7),
                         reads=wk(k) + [("hT", par, k)], writes=[kb])
            src = b[:, 0:nb * 128].rearrange("p (q t) -> p q t", q=nb)
            if g0 < 8:
                S.act(ACTF(qT[0:64, g0:g0 + 4, 0, :], src[0:64], AF.Copy, scale=0.125), reads=[kb], writes=[("qT", par, g0)])
                S.act(ACTF(qT[64:128, g0:g0 + 4, 1, :], src[64:128], AF.Copy, scale=0.125), reads=[kb], writes=[("qT", par, g0)])
            else:
                S.copy(kT2[:, :, cur, :], src, reads=[kb], writes=[("kT", cur)])
        vb, vkb = bank()
        for k in range(8):
            S.pe(MM(vb[:, 0:128], hT[:, k, :], W[:, k, 1280:1408], k == 0, k == 7), reads=wk(k) + [("hT", par, k)], writes=[vkb])
        S.copy(V1[:, cur, :, 0:64], vb[:, 0:128].rearrange("p (g d) -> p g d", g=2), reads=[vkb], writes=[("V1", cur)])
        for zc in range(2):
            zb, zkb = bank()
            for k in range(8):
                S.pe(MM(zb[:], hT[:, k, :], W[:, k, 1408 + zc * 512:1920 + zc * 512], k == 0, k == 7),
                     reads=wk(k) + [("hT", par, k)], writes=[zkb])
            S.act(ACTF(tzt[:, 0, zc * 512:(zc + 1) * 512], zb[:], AF.Tanh, scale=0.5), reads=[zkb], writes=[("tz", par, zc)])
            S.dve(STT(zst[:, 1, zc * 512:(zc + 1) * 512], tzt[:, 0, zc * 512:(zc + 1) * 512], 1.0, zb[:], ALU.add, ALU.mult),
                  reads=[zkb, ("tz", par, zc)], writes=[("zs", par, zc)])
        for g in range(2):
            for pc in ((1, 0) if c > 0 else (1,)):
                ring = cur if pc == 1 else prv
                for bi in range(2):
                    b, kb = bank()
                    for al in range(2):
                        j = 4 * g + 2 * bi + al
                        S.pe(MM(b[:, al * 256:(al + 1) * 256], kT2[:, g, ring, :], qT[:, j, :, :].rearrange("p a q -> p (a q)")),
                             reads=[("kT", ring), ("qT", par, 0), ("qT", par, 4)], writes=[kb])
                    h0 = 8 * g + 4 * bi
                    sl = n_ex % 4
                    n_ex += 1
                    S.act(ACTF(ex[:, sl, :], b[:], AF.Exp), reads=[kb], writes=[("ex", sl), ("ex2", sl)])
                    S.dve(TT(PT[:, pc, h0:h0 + 4, :], ex[:, sl, :].rearrange("p (a q) -> p a q", a=4),
                             E[:, pc, h0:h0 + 4, :], ALU.mult), reads=[("ex", sl), ("ex2", sl)] + Ek, writes=[("PT", par, g, pc, bi)])
        for hb in range(4):
            b, kb = bank()
            for a in range(4):
                h = hb * 4 + a
                g = h // 8
                pcs = (1, 0) if c > 0 else (1,)
                for n_, pc in enumerate(pcs):
                    ring = cur if pc == 1 else prv
                    S.pe(MM(b[:, a * 65:(a + 1) * 65], PT[:, pc, h, :], V1[:, ring, g, :], n_ == 0, n_ == len(pcs) - 1),
                         reads=[("PT", par, g, pc, (h % 8) // 4), ("V1", ring)], writes=[kb])
            b3 = b[:, 0:260].rearrange("p (a e) -> p a e", a=4)
            S.dve(TT(den[:, hb * 4:hb * 4 + 4], b3[:, :, 64], esink[:, hb * 4:hb * 4 + 4], ALU.add), reads=[kb, "esink"], writes=[("den", par, hb)])
            S.dve(_wc(lambda e, hb=hb, den=den: e.reciprocal(out=den[:, hb * 4:hb * 4 + 4], in_=den[:, hb * 4:hb * 4 + 4]), 0.15),
                  reads=[("den", par, hb)], writes=[("den", par, hb)])
            S.dve(TT(att[:, hb * 256:(hb + 1) * 256].rearrange("p (a d) -> p a d", a=4), b3[:, :, 0:64],
                     bcl(den[:, hb * 4:hb * 4 + 4], 64), ALU.mult), reads=[kb, ("den", par, hb)], writes=[("att", par, hb)])
        for zc in range(2):
            S.dp(TT(att[:, zc * 512:(zc + 1) * 512], att[:, zc * 512:(zc + 1) * 512], zst[:, 1, zc * 512:(zc + 1) * 512], ALU.mult),
                   reads=[("att", par, 2 * zc), ("att", par, 2 * zc + 1), ("zs", par, zc)], writes=[("att", par, 2 * zc), ("att", par, 2 * zc + 1)])
            S.act(ACTF(y[:, zc * 512:(zc + 1) * 512], att[:, zc * 512:(zc + 1) * 512], AF.Copy, scale=0.5),
                  reads=[("att", par, 2 * zc), ("att", par, 2 * zc + 1)], writes=[("y", par)])
        out_proj_residual(S, C, Wo, slot, kxs, par)
        xn, sm = C.xns[par], C.sms[par]
        S.act(ACTF(att[:], xs, AF.Square), reads=[kxs] + [("att", par, q) for q in range(4)], writes=[("att", par, q) for q in range(4)])
        S.dve(RED(sm[:, 4:5], att[:]), reads=[("att", par, q) for q in range(4)], writes=[("sm4", par)])
        S.dve(TS(sm[:, 5:6], sm[:, 4:5], 1.0 / D, 1e-6, ALU.mult, ALU.add), reads=[("sm4", par)], writes=[("sm5", par)])
        S.pool(TT(sm[:, 6:7], sm[:, 5:6], C.mhalf[:, 0:1], ALU.pow), reads=[("sm5", par), "mhalf"], writes=[("sm6", par)])
        S.dve(STT(ob[:], xs, sm[:, 6:7], fg, ALU.mult, ALU.mult), reads=[kxs, ("sm6", par), "cbc"], writes=[("ob", par)])
        out_ops.append(S.dma(DMA(x_out[s, c * 128:(c + 1) * 128, :], ob[:]), reads=[("ob", par)], writes=[("out", s, c)], chan=f"oo{par}"))
    return out_ops


def build_B():
    nc = bass.Bass("TRN2", target_bir_lowering=False)
    dr = {}
    dr["x1"] = nc.dram_tensor("x1", [2, SEQ, D], F32, kind="ExternalInput").ap()
    dr["cT"] = nc.dram_tensor("cT", [128, 8, 2], F32, kind="ExternalInput").ap()
    dr["ada_w"] = nc.dram_tensor("ada_w1", [D, 3 * D], F32, kind="ExternalInput").ap()
    dr["o_w_in"] = nc.dram_tensor("o_w_in", [D, W1C], F32, kind="ExternalInput").ap()
    dr["o_w_out"] = nc.dram_tensor("o_w_out", [D, D], F32, kind="ExternalInput").ap()
    dr["cfm"] = nc.dram_tensor("cfm", [128, NC_FM], F32, kind="ExternalInput").ap()
    dr["cbc"] = nc.dram_tensor("cbcB", [128, 1040], F32, kind="ExternalInput").ap()
    out = nc.dram_tensor("out", [2, SEQ, D], F32, kind="ExternalOutput").ap()
    S = Sched(nc)
    with contextlib.ExitStack() as st:
        outs = phaseB(nc, S, st, dr, out)
        S.emit(final_wait_ops=outs[-2:])
    return nc


def host_w1(inp):
    w = np.asarray(inp["o_w_in"][0], np.float32)
    q, k, v, z = w[:, 0:1024], w[:, 1024:1152], w[:, 1152:1280], w[:, 1280:2304]
    k0 = np.concatenate([k[:, 0:64], k[:, 0:64]], axis=1)
    k1 = np.concatenate([k[:, 64:128], k[:, 64:128]], axis=1)
    return np.ascontiguousarray(np.concatenate([q, k0, k1, v, z], axis=1))


def run_B(inp, x1):
    if "B" not in _CACHE:
        _CACHE["B"] = build_B()
    nc = _CACHE["B"]
    cfm, cbc, lora, sgwT = host_consts(inp)
    c = np.asarray(inp["c"], np.float32)
    w1 = host_w1(inp)
    maps = []
    for i in range(NCORES):
        cs = c[2 * i:2 * i + 2]
        cT = np.ascontiguousarray(cs.reshape(2, 8, 128).transpose(2, 1, 0))
        maps.append({"x1": np.ascontiguousarray(x1[2 * i:2 * i + 2]), "cT": cT,
                     "ada_w1": np.ascontiguousarray(inp["ada_w"][1]), "o_w_in": w1,
                     "o_w_out": np.ascontiguousarray(inp["o_w_out"][0]), "cfm": cfm,
                     "cbcB": np.ascontiguousarray(cbc[:, 2048:3088])})
    res = run_bass_kernel_spmd(nc, maps, core_ids=list(range(NCORES)))
    return np.concatenate([r["out"] for r in res.results], axis=0)


def build_fused(plan=True):
    if plan and not BANK_ASSIGN:
        _plan_pass()
    return _build_fused_real()


def _plan_pass():
    nc = bass.Bass("TRN2", target_bir_lowering=False)
    inp = lambda n, shp: nc.dram_tensor(n, shp, F32, kind="ExternalInput").ap()
    drA = {"x": inp("x", [2, SEQ, D]), "cT": inp("cT", [128, 8, 2]), "ada_w": inp("ada_w0", [D, 3 * D]),
           "e_w_in": inp("e_w_in", [D, 3712]), "e_w_out": inp("e_w_out", [D, D]), "cfm": inp("cfm", [128, NC_FM]),
           "cbc": inp("cbcA", [128, 2048]), "lora": inp("lora", [128, 512]), "sgwT": inp("sgwT", [128, 8, 128])}
    x1 = nc.dram_tensor("x1", [2, SEQ, D], F32, kind="Internal").ap()
    drB = {"x1": x1, "cT": drA["cT"], "ada_w": inp("ada_w1", [D, 3 * D]), "o_w_in": inp("o_w_in", [D, W1C]),
           "o_w_out": inp("o_w_out", [D, D]), "cfm": drA["cfm"], "cbc": inp("cbcB", [128, 1040])}
    out = nc.dram_tensor("out", [2, SEQ, D], F32, kind="ExternalOutput").ap()
    DUMP["ops"] = []
    S = Sched(nc)
    S.plan = []
    with contextlib.ExitStack() as st:
        phaseA(nc, S, st, drA, x1)
        S._flush()
    with contextlib.ExitStack() as st:
        phaseB(nc, S, st, drB, out)
        S._flush()
    BANK_ASSIGN["a"], BANK_ASSIGN["b"] = S.plan[0], S.plan[1]


def _build_fused_real():
    nc = bass.Bass("TRN2", target_bir_lowering=False)
    inp = lambda n, shp: nc.dram_tensor(n, shp, F32, kind="ExternalInput").ap()
    drA = {"x": inp("x", [2, SEQ, D]), "cT": inp("cT", [128, 8, 2]), "ada_w": inp("ada_w0", [D, 3 * D]),
           "e_w_in": inp("e_w_in", [D, 3712]), "e_w_out": inp("e_w_out", [D, D]), "cfm": inp("cfm", [128, NC_FM]),
           "cbc": inp("cbcA", [128, 2048]), "lora": inp("lora", [128, 512]), "sgwT": inp("sgwT", [128, 8, 128])}
    x1 = nc.dram_tensor("x1", [2, SEQ, D], F32, kind="Internal").ap()
    drB = {"x1": x1, "cT": drA["cT"], "ada_w": inp("ada_w1", [D, 3 * D]), "o_w_in": inp("o_w_in", [D, W1C]),
           "o_w_out": inp("o_w_out", [D, D]), "cfm": drA["cfm"], "cbc": inp("cbcB", [128, 1040])}
    out = nc.dram_tensor("out", [2, SEQ, D], F32, kind="ExternalOutput").ap()
    DUMP["ops"] = []
    S = Sched(nc)
    if "a" in BANK_ASSIGN:
        S.bank_plans = [BANK_ASSIGN["a"], BANK_ASSIGN["b"]]
    with contextlib.ExitStack() as sems:
        with contextlib.ExitStack() as st:
            phaseA(nc, S, st, drA, x1)
            S.barrier()
            S.emit(sem_stack=sems)
        with contextlib.ExitStack() as st:
            outs = phaseB(nc, S, st, drB, out)
            S.emit(final_wait_ops=outs[-2:], sem_stack=sems)
    return nc


def kernel(**inputs):
    inp = inputs
    if "F" not in _CACHE:
        _CACHE["F"] = build_fused()
    nc = _CACHE["F"]
    cfm, cbc, lora, sgwT = host_consts(inp)
    x = np.asarray(inp["x"], np.float32)
    c = np.asarray(inp["c"], np.float32)
    w1 = host_w1(inp)
    shared = {"ada_w0": np.ascontiguousarray(inp["ada_w"][0]), "e_w_in": np.ascontiguousarray(inp["e_w_in"][0]),
              "e_w_out": np.ascontiguousarray(inp["e_w_out"][0]), "cfm": cfm, "cbcA": np.ascontiguousarray(cbc[:, 0:2048]),
              "lora": lora, "sgwT": sgwT, "ada_w1": np.ascontiguousarray(inp["ada_w"][1]), "o_w_in": w1,
              "o_w_out": np.ascontiguousarray(inp["o_w_out"][0]), "cbcB": np.ascontiguousarray(cbc[:, 2048:3088])}
    maps = []
    for i in range(NCORES):
        cs = c[2 * i:2 * i + 2]
        cT = np.ascontiguousarray(cs.reshape(2, 8, 128).transpose(2, 1, 0))
        m = dict(shared)
        m["x"] = np.ascontiguousarray(x[2 * i:2 * i + 2])
        m["cT"] = cT
        maps.append(m)
    res = run_bass_kernel_spmd(nc, maps, core_ids=list(range(NCORES)))
    return np.concatenate([r["out"] for r in res.results], axis=0).astype(np.float32)
```
